# Optimizing a Trainium2 kernel written in Bass

```python
import jax, jax.numpy as jnp
from jax import lax
import numpy as np

D_MODEL = 1024
BATCH = 2
SEQ = 16384
DEPTH = 4
DEC_BATCH = 8
DEC_SEQ = 32
PAST_LEN = 4096

CHUNK = 64
D_MIX = D_MODEL
D_A = D_MIX // 2
N_A_HEADS = 4
A_HEAD_DIM = D_A // N_A_HEADS
GMLP_CHUNK = 128
D_B = D_MIX // 4
POOL_WINDOWS = (2, 4, 8, 16)
N_POOL_GROUPS = len(POOL_WINDOWS)
POOL_GROUP = D_B // N_POOL_GROUPS
POOL_HIST = max(POOL_WINDOWS) - 1
D_C = D_MIX - D_A - D_B
CONV_W = 3
CONV_HIST = CONV_W - 1
D_IN = 2 * D_A + D_B + 3 * D_C
D_FF = 11 * D_MODEL // 4
EPS = 1e-6

kernel_name = 'hybrid_stream_gmlp_pool_shortconv'


def _rmsnorm(x, g):
    xf = x.astype(jnp.float32)
    y = xf * lax.rsqrt(jnp.mean(xf * xf, axis=-1, keepdims=True) + EPS)
    return (y * g.astype(jnp.float32)).astype(x.dtype)


def _causal_dwconv3(ext, w):
    return w[0] * ext[:, :-2] + w[1] * ext[:, 1:-1] + w[2] * ext[:, 2:]


def _chunk_spatial_gate(u, v, w_s, b_s):
    b, T, _ = v.shape
    L = min(T, GMLP_CHUNK)
    n = T // L
    mask = jnp.tril(jnp.ones((L, L), dtype=w_s.dtype))
    w = w_s[:, :L, :L] * mask
    vh = v.reshape(b, n, L, N_A_HEADS, A_HEAD_DIM)
    s = jnp.einsum('hqk,bnkhd->bnqhd', w, vh) + b_s[:, :L].T[:, :, None]
    return u * s.reshape(b, T, D_A)


def _multiscale_pool(p_ext, pos0, w_pool, scale):
    b, Lx, _ = p_ext.shape
    T = Lx - POOL_HIST
    pf = p_ext.astype(jnp.float32)
    cs = jnp.concatenate([jnp.zeros((b, 1, D_B), jnp.float32), lax.cumsum(pf, axis=1)], axis=1)
    end = cs[:, POOL_HIST + 1:]
    pos = pos0 + jnp.arange(T)
    means = []
    for gi, w in enumerate(POOL_WINDOWS):
        sl = slice(gi * POOL_GROUP, (gi + 1) * POOL_GROUP)
        start = cs[:, POOL_HIST + 1 - w: POOL_HIST + 1 - w + T, sl]
        cnt = jnp.minimum(pos + 1, w).astype(jnp.float32)[None, :, None]
        means.append((end[..., sl] - start) / cnt)
    d = (jnp.concatenate(means, axis=-1) - pf[:, POOL_HIST:]).reshape(b, T, N_POOL_GROUPS, POOL_GROUP)
    y = jnp.einsum('btgc,gcd->btgd', d, w_pool.astype(jnp.float32)).reshape(b, T, D_B)
    return (y * scale.astype(jnp.float32)).astype(p_ext.dtype)


def _layer(x, pool_hist, conv_hist, ffn_hist, pos0, g1, w_in, w_s, b_s, w_pool, pool_scale,
           w_conv, w_out, g2, w_up, w_fconv, b_fconv, w_down):
    h = _rmsnorm(x, g1)
    z = h @ w_in
    cuts = [D_A, 2 * D_A, 2 * D_A + D_B, 2 * D_A + D_B + D_C, 2 * D_A + D_B + 2 * D_C]
    u_a, v_a, p_b, gate_b, gate_c, h_c = jnp.split(z, cuts, axis=-1)
    y_a = _chunk_spatial_gate(u_a, v_a, w_s, b_s)
    p_ext = jnp.concatenate([pool_hist, p_b], axis=1)
    y_b = _multiscale_pool(p_ext, pos0, w_pool, pool_scale)
    q_ext = jnp.concatenate([conv_hist, gate_c * h_c], axis=1)
    y_c = gate_b * _causal_dwconv3(q_ext, w_conv)
    x = x + jnp.concatenate([y_a, y_b, y_c], axis=-1) @ w_out
    h2 = _rmsnorm(x, g2)
    up_ext = jnp.concatenate([ffn_hist, h2 @ w_up], axis=1)
    upc = _causal_dwconv3(up_ext, w_fconv) + b_fconv
    g, a = jnp.split(upc, 2, axis=-1)
    x = x + (jax.nn.silu(g) * a) @ w_down
    return x, p_ext[:, -POOL_HIST:], q_ext[:, -CONV_HIST:], up_ext[:, -CONV_HIST:], v_a


def setup_inputs(seed: int = 0) -> dict:
    key = jax.random.key(seed)
    ks = jax.random.split(key, 24)
    f32 = jnp.float32
    nrm = lambda k, s, sc: jax.random.normal(k, s, f32) * sc
    return {
        'x_prompt': nrm(ks[0], (BATCH, SEQ, D_MODEL), 1.0),
        'x_sample': nrm(ks[1], (DEC_BATCH, DEC_SEQ, D_MODEL), 1.0),
        'state_pool': nrm(ks[2], (DEPTH, DEC_BATCH, POOL_HIST, D_B), 1.0),
        'state_conv': nrm(ks[3], (DEPTH, DEC_BATCH, CONV_HIST, D_C), 1.0),
        'state_ffn_conv': nrm(ks[4], (DEPTH, DEC_BATCH, CONV_HIST, 2 * D_FF), 1.0),
        'norm1_g': 1.0 + nrm(ks[5], (DEPTH, D_MODEL), 0.02),
        'w_in': nrm(ks[6], (DEPTH, D_MODEL, D_IN), D_MODEL ** -0.5),
        'w_s': nrm(ks[7], (DEPTH, N_A_HEADS, GMLP_CHUNK, GMLP_CHUNK), 0.5 * GMLP_CHUNK ** -0.5),
        'b_s': 1.0 + nrm(ks[8], (DEPTH, N_A_HEADS, GMLP_CHUNK), 0.1),
        'w_pool': nrm(ks[9], (DEPTH, N_POOL_GROUPS, POOL_GROUP, POOL_GROUP), POOL_GROUP ** -0.5),
        'pool_scale': 1.0 + nrm(ks[10], (DEPTH, D_B), 0.1),
        'w_conv': nrm(ks[11], (DEPTH, CONV_W, D_C), CONV_W ** -0.5),
        'w_out': nrm(ks[12], (DEPTH, D_MIX, D_MODEL), D_MIX ** -0.5),
        'norm2_g': 1.0 + nrm(ks[13], (DEPTH, D_MODEL), 0.02),
        'w_up': nrm(ks[14], (DEPTH, D_MODEL, 2 * D_FF), D_MODEL ** -0.5),
        'w_fconv': nrm(ks[15], (DEPTH, CONV_W, 2 * D_FF), CONV_W ** -0.5),
        'b_fconv': nrm(ks[16], (DEPTH, 2 * D_FF), 0.02),
        'w_down': nrm(ks[17], (DEPTH, D_FF, D_MODEL), D_FF ** -0.5),
        'final_g': 1.0 + nrm(ks[18], (D_MODEL,), 0.02),
    }


def reference(x_prompt, x_sample, state_pool, state_conv, state_ffn_conv, norm1_g, w_in, w_s, b_s,
              w_pool, pool_scale, w_conv, w_out, norm2_g, w_up, w_fconv, b_fconv, w_down, final_g):
    xp, xs = x_prompt, x_sample
    bp = xp.shape[0]
    zero_pool = jnp.zeros((bp, POOL_HIST, D_B), xp.dtype)
    zero_conv = jnp.zeros((bp, CONV_HIST, D_C), xp.dtype)
    zero_ffn = jnp.zeros((bp, CONV_HIST, 2 * D_FF), xp.dtype)
    pool_p, conv_p, ffn_p = [], [], []
    pool_s, conv_s, ffn_s, chunk_v_s = [], [], [], []
    for l in range(DEPTH):
        params = (norm1_g[l], w_in[l], w_s[l], b_s[l], w_pool[l], pool_scale[l], w_conv[l], w_out[l],
                  norm2_g[l], w_up[l], w_fconv[l], b_fconv[l], w_down[l])
        xp, pp, cp, fp, _ = _layer(xp, zero_pool, zero_conv, zero_ffn, 0, *params)
        xs, ps, cs, fs, vs = _layer(xs, state_pool[l], state_conv[l], state_ffn_conv[l], PAST_LEN, *params)
        pool_p.append(pp); conv_p.append(cp); ffn_p.append(fp)
        pool_s.append(ps); conv_s.append(cs); ffn_s.append(fs); chunk_v_s.append(vs)
    y_prompt = _rmsnorm(xp, final_g)
    y_sample = _rmsnorm(xs, final_g)
    return (y_prompt, y_sample, jnp.stack(pool_p), jnp.stack(conv_p), jnp.stack(ffn_p),
            jnp.stack(pool_s), jnp.stack(conv_s), jnp.stack(ffn_s), jnp.stack(chunk_v_s))
```

```python
import numpy as np
import concourse.bass as bass
import concourse.mybir as mybir
from concourse.bass_utils import run_bass_kernel_spmd

F32 = mybir.dt.float32
BF16 = mybir.dt.bfloat16
ALU = mybir.AluOpType
AF = mybir.ActivationFunctionType

D = 1024
KC = 8
DEPTH = 4
DFF = 2816
NJ = 22
NCH = 44
HALO = 384
NSLOT = 8
PF = 7
STAGGER = True
PRESQ = True
TRIM = True
HOFF = (0, 128, 128, 256)
MUL_ENG = "pool"
CHAIN_ENG = "pool"
SLOT = KC * 256
HK = NJ // 2
EPS = 1e-6
SAMP = 32

O_G1 = 0
O_G2 = O_G1 + DEPTH * KC
O_FG = O_G2 + DEPTH * KC
O_PSC = O_FG + KC
O_WCV = O_PSC + DEPTH * 2
O_WF = O_WCV + DEPTH * 2 * 3
O_BF = O_WF + DEPTH * NCH * 3
O_INVW = O_BF + DEPTH * NCH
O_INVC = O_INVW + 2
O_FLAG = O_INVC + 32
O_EPS = O_FLAG + 1
O_M01 = O_EPS + 1
NCST = O_M01 + 2


class Sched:
    def __init__(self):
        self.progs = {e: [] for e in ("pe", "act", "dve", "pool", "sp")}
        self.count = {}
        self.waited = {e: {} for e in self.progs}
        self.res_w = {}
        self.res_r = {}
        self.cur_phase = {}
        self.prev_phase = {}

    def add_sem(self, name):
        self.count[name] = 0

    def phase_switch(self):
        for s, v in self.cur_phase.items():
            if self.prev_phase.get(s, 0) < v:
                self.prev_phase[s] = v
        self.cur_phase = {}

    def op(self, eng, fn, reads=(), writes=(), sem=None, inc=1, region=False):
        deps = {}

        def add(tok):
            if tok is None:
                return
            s, v = tok
            if deps.get(s, 0) < v:
                deps[s] = v

        for r in reads:
            add(self.res_w.get(r))
        for w in writes:
            add(self.res_w.get(w))
            for s, v in self.res_r.get(w, {}).items():
                add((s, v))
        if region:
            for s, v in self.prev_phase.items():
                add((s, v))
        prog = self.progs[eng]
        waited = self.waited[eng]
        for s, v in deps.items():
            if waited.get(s, 0) < v:
                waited[s] = v
                prog.append(("wait", s, v))
        semname = sem or eng
        self.count[semname] += inc
        tok = (semname, self.count[semname])
        prog.append(("op", fn, semname, inc))
        for r in reads:
            d = self.res_r.setdefault(r, {})
            if d.get(tok[0], 0) < tok[1]:
                d[tok[0]] = tok[1]
        for w in writes:
            self.res_w[w] = tok
            self.res_r[w] = {}
        if region:
            if self.cur_phase.get(tok[0], 0) < tok[1]:
                self.cur_phase[tok[0]] = tok[1]
        return tok

    def wait_all(self, eng, semnames):
        prog = self.progs[eng]
        for s in semnames:
            v = self.count[s]
            if v > 0 and self.waited[eng].get(s, 0) < v:
                self.waited[eng][s] = v
                prog.append(("wait", s, v))


class Seg:
    pass


def subtiles(T, maxn=448):
    out = []
    c = 0
    while c < T:
        n = min(maxn, T - c)
        out.append((c, c + n))
        c += n
    return out


def build(main_tok, tiles, with_sample=True):
    NTOK = HALO + main_tok
    assert sum(tiles) == NTOK and all(t % 128 == 0 for t in tiles)
    TMAX = max(tiles)
    NT = len(tiles)
    nc = bass.Bass("TRN2", target_bir_lowering=False)
    S = Sched()

    def din(name, shape):
        return nc.dram_tensor(name, list(shape), F32, kind="ExternalInput").ap()

    def dout(name, shape):
        return nc.dram_tensor(name, list(shape), F32, kind="ExternalOutput").ap()

    xT = din("xT", [128, KC, NTOK])
    xsT = din("xsT", [128, KC, SAMP])
    cst_d = din("cst", [128, NCST])
    wsT_d = din("wsT", [128, DEPTH * 4, 128])
    mask_d = din("mask", [128, 128])
    bs2_d = din("bs2", [2, DEPTH * 4 * 128])
    wpool_d = din("wpool", [128, DEPTH * 2, 128])
    ident_d = din("ident", [128, 128])
    spool_d = din("spool", [128, DEPTH, 2, 16])
    sconv_d = din("sconv", [128, DEPTH, 2, 2])
    sffn_d = din("sffn", [128, DEPTH, NCH, 2])
    w_in_d = din("w_in_r", [DEPTH, 12, 128, KC * 128])
    w_v_d = din("w_v_r", [DEPTH, 2, 128, KC * 256])
    w_out_d = din("w_out_r", [DEPTH, 8, 128, KC * 128])
    w_up_d = din("w_up_r", [DEPTH, NJ, 128, KC * 256])
    w_dn_d = din("w_dn_r", [DEPTH, 8, 128, NJ * 128])

    yT_d = dout("yT", [128, KC, main_tok])
    ysT_d = dout("ysT", [128, KC, SAMP])
    o_pool = {"m": dout("pool_p", [128, DEPTH, 2, 16]), "s": dout("pool_s", [128, DEPTH, 2, 16])}
    o_conv = {"m": dout("conv_p", [128, DEPTH, 2, 2]), "s": dout("conv_s", [128, DEPTH, 2, 2])}
    o_ffn = {"m": dout("ffn_p", [128, DEPTH, NCH, 2]), "s": dout("ffn_s", [128, DEPTH, NCH, 2])}
    o_v = dout("v_s", [SAMP, DEPTH, 512])

    def sb(name, shape, dt):
        return nc.alloc_sbuf_tensor(name, list(shape), dt)

    cst = sb("cst_t", [128, NCST], F32)
    wsT = sb("wsT_b", [128, DEPTH * 4, 128], BF16)
    bs2 = sb("bs2_b", [2, DEPTH * 4 * 128], BF16)
    bs2v = bs2[:, :].rearrange("p (i q) -> p i q", i=DEPTH * 4)
    wpool = sb("wpool_b", [128, DEPTH * 2, 128], BF16)
    ident = sb("ident_t", [128, 128], F32)
    ones_b = sb("ones_b", [128, 128], BF16)
    ring = sb("ring", [128, NSLOT, SLOT], BF16)

    def mkseg(name, T, maxn):
        g = Seg()
        g.name = name
        g.T = T
        g.maxn = maxn
        g.xb = [sb(name + "_x0", [128, KC, T], F32)]
        g.xb.append(sb(name + "_x1", [128, KC, T], F32) if name == "m" else g.xb[0])
        g.xcur = g.xb[0]
        g.xn = "x0"
        g.off = 0
        g.h = sb(name + "_h", [128, KC, 2 + T], BF16)
        g.ycat = sb(name + "_yc", [128, KC, T], BF16)
        g.rstd = sb(name + "_rs", [128, T], F32)
        g.phist = sb(name + "_ph", [128, DEPTH, 2, 16], F32)
        g.qhist = sb(name + "_qh", [128, DEPTH, 2, 2], F32)
        g.hhist = sb(name + "_hh", [128, DEPTH, KC, 2], BF16)
        g.upst = sb(name + "_us", [128, DEPTH, NCH, 2], F32)
        return g

    NF = 3
    FW = 448
    main = mkseg("m", TMAX, 448)
    T = TMAX
    mixer_bytes = 4 * 4 * T + 2 * (T // 128) * 512 + 3 * 4 * 2 * (16 + T) + 2 * 2 * T + 4 * 2 * T + 4 * 2 * (2 + T)
    ffn_bytes = 2 * NJ * T + 3 * NF * 4 * FW
    y_bytes = 4 * KC * T
    ub = max(mixer_bytes, ffn_bytes, y_bytes)
    ub = (ub + 63) // 64 * 64
    union = sb("union", [128, ub // 4], F32)

    def carve(seg_bufs, base_tensor, dt_bytes_total):
        pass

    class Carver:
        def __init__(self, t):
            self.t = t
            self.off = 0

        def reset(self):
            self.off = 0

        def take(self, shape, dt):
            esz = 4 if dt == F32 else 2
            n = int(np.prod(shape[1:]))
            self.off = (self.off + 31) // 32 * 32
            o = self.off
            self.off += n * esz
            assert self.off <= ub, (self.off, ub)
            if dt == F32:
                ap = self.t[:, o // 4:o // 4 + n]
            else:
                ap = self.t[:].bitcast(BF16)[:, o // 2:o // 2 + n]
            if len(shape) == 3:
                ap = ap.rearrange("p (a b) -> p a b", a=shape[1])
            return ap

    cv = Carver(union)
    nchunk_m = T // 128
    main.u = cv.take([128, 4, T], F32)
    main.vt = cv.take([128, nchunk_m, 512], BF16)
    main.P = cv.take([128, 2, 16 + T], F32)
    main.A1 = cv.take([128, 2, 16 + T], F32)
    main.A2 = cv.take([128, 2, 16 + T], F32)
    main.d = cv.take([128, 2, T], BF16)
    main.acc = cv.take([128, 2, T], F32)
    main.Q = cv.take([128, 2, 2 + T], F32)
    cv.reset()
    main.hid = cv.take([128, NJ, T], BF16)
    fscr = [[cv.take([128, FW], F32) for _ in range(NF)] for _ in range(3)]
    cv.reset()
    wsT_f = cv.take([128, DEPTH * 4, 128], F32)
    mask_t = cv.take([128, 128], F32)
    bs_f = cv.take([128, DEPTH * 4 * 128], F32)[0:2, :]
    bs_h = cv.take([128, DEPTH * 4 * 128], BF16)[0:2, :]
    bs_r = cv.take([128, DEPTH * 4 * 128], F32)[0:2, :]
    main.region = True
    main.vf = None

    samp = None
    if with_sample:
        samp = mkseg("s", SAMP, SAMP)
        samp.u = sb("s_u", [128, 4, SAMP], F32)
        samp.vt = sb("s_vt", [128, 1, 512], BF16)
        samp.vf = sb("s_vf", [SAMP, 512], F32)
        samp.P = sb("s_P", [128, 2, 16 + SAMP], F32)
        samp.A1 = sb("s_A1", [128, 2, 16 + SAMP], F32)
        samp.A2 = sb("s_A2", [128, 2, 16 + SAMP], F32)
        samp.d = sb("s_d", [128, 2, SAMP], BF16)
        samp.acc = sb("s_acc", [128, 2, SAMP], F32)
        samp.Q = sb("s_Q", [128, 2, 2 + SAMP], F32)
        samp.hid = sb("s_hid", [128, NJ, SAMP], BF16)
        samp.region = False

    ps = nc.alloc_psum_tensor("ps", [128, 8, 512], F32) if hasattr(nc, "alloc_psum_tensor") else None
    assert ps is not None

    sem_names = ["pe", "act", "dve", "pool", "init", "initg", "xld0", "xld1", "misc", "vst"] + ["ring%d" % i for i in range(NSLOT)] + \
                ["st%d" % i for i in range(4)]
    for n in sem_names:
        S.add_sem(n)

    def C(off, n=1):
        return cst[:, off:off + n]

    bank_ctr = [0]

    def next_bank():
        b = bank_ctr[0] % 8
        bank_ctr[0] += 1
        return b

    fs_ctr = [0]

    wchunk_ctr = [0]
    wissued = [0]
    wseq = []
    for _ti in range(NT):
        _nu = len(subtiles(tiles[_ti], 448)) + (1 if (with_sample and _ti == NT - 1) else 0)
        _np = 2 if (STAGGER and _nu > 1) else 1
        for _l in range(DEPTH):
            for _w in (8, 9, 10, 11, 4, 5):
                wseq.append((w_in_d[_l, _w], KC * 128))
            for _hf in range(2):
                wseq.append((w_v_d[_l, _hf], KC * 256))
            for _w in (0, 1, 2, 3, 6, 7):
                wseq.append((w_in_d[_l, _w], KC * 128))
            for _p in range(_np):
                for _m in range(8):
                    wseq.append((w_out_d[_l, _m], KC * 128))
            for _j in range(NJ):
                wseq.append((w_up_d[_l, _j], KC * 256))
            for _p in range(_np):
                for _m in range(8):
                    for _hh in range(2):
                        wseq.append((w_dn_d[_l, _m][:, _hh * HK * 128:(_hh + 1) * HK * 128], HK * 128))

    def wissue_upto(n):
        while wissued[0] < min(n, len(wseq)):
            i = wissued[0]
            wissued[0] += 1
            slot = i % NSLOT
            src, nelem = wseq[i]
            dst = ring[:, slot, 0:nelem]
            S.op("pool", lambda e, dst=dst, src=src: e.dma_start(out=dst, in_=src),
                 writes=["ring%d" % slot], sem="ring%d" % slot, inc=16)

    wopen = []

    def wrelease():
        del wopen[:]
        wissue_upto(wchunk_ctr[0] + PF)

    def wload(dram_ap, nelem, keep=False):
        i = wchunk_ctr[0]
        wchunk_ctr[0] += 1
        assert wseq[i][1] == nelem, (i, wseq[i][1], nelem)
        if not keep:
            del wopen[:]
        wopen.append(i)
        assert i - wopen[0] < NSLOT
        wissue_upto(min(i + PF, wopen[0] + NSLOT))
        slot = i % NSLOT
        return slot, ring[:, slot, 0:nelem]

    def mm_group(mms, reads, writes, region=False):
        def fn(e, mms=mms):
            ins = None
            for (o, l, r, st, sp) in mms:
                ins = e.matmul(o, l, r, start=st, stop=sp)
            return ins
        return S.op("pe", fn, reads=reads, writes=writes, region=region)

    def init():
        loads = [(cst[:], cst_d), (wsT_f, wsT_d), (mask_t, mask_d), (bs_f, bs2_d), (ident[:], ident_d)]
        for (dst, src) in loads:
            S.op("sp", lambda e, dst=dst, src=src: e.dma_start(out=dst, in_=src), writes=[], sem="init", inc=16)
        S.op("pool", lambda e: e.dma_start(out=wpool[:], in_=wpool_d), writes=[], sem="initg", inc=16)
        for g in ([main] + ([samp] if samp else [])):
            if g is main:
                S.op("pool", lambda e, g=g: e.memset(g.phist[:], 0.0), writes=[])
                S.op("pool", lambda e, g=g: e.memset(g.qhist[:], 0.0), writes=[])
            else:
                S.op("sp", lambda e, g=g: e.dma_start(out=g.phist[:], in_=spool_d), writes=[], sem="init", inc=16)
                S.op("sp", lambda e, g=g: e.dma_start(out=g.qhist[:], in_=sconv_d), writes=[], sem="init", inc=16)
                S.op("sp", lambda e, g=g: e.dma_start(out=g.upst[:], in_=sffn_d), writes=[], sem="init", inc=16)
                S.op("sp", lambda e, g=g: e.dma_start(out=g.xb[0][:], in_=xsT), writes=[], sem="init", inc=16)
            S.op("pool", lambda e, g=g: e.memset(g.hhist[:], 0.0), writes=[])
        S.op("pool", lambda e: e.memset(ones_b[:], 1.0), writes=[])
        for eng in ("pe", "act", "dve", "pool"):
            S.wait_all(eng, ["init", "initg", "pool"])
        for i in range(DEPTH * 4):
            S.op("dve", lambda e, i=i: e.tensor_tensor(out=wsT[:, i, :], in0=wsT_f[:, i, :], in1=mask_t,
                                                     op=ALU.mult), writes=["wsT"], region=True)
        S.op("dve", lambda e: e.tensor_copy(out=bs_h, in_=bs_f), writes=["bs_h"], region=True)
        S.op("dve", lambda e: e.tensor_copy(out=bs_r, in_=bs_h), reads=["bs_h"], writes=["bs_r"], region=True)
        S.op("dve", lambda e: e.tensor_tensor(out=bs_r, in0=bs_f, in1=bs_r, op=ALU.subtract),
             reads=["bs_r"], writes=["bs_r"], region=True)
        S.op("dve", lambda e: e.tensor_scalar(out=bs_f, in0=bs_h, scalar1=cst[0:2, O_M01:O_M01 + 1],
                                              scalar2=None, op0=ALU.mult), reads=["bs_h", "bs_r"], writes=["bs_f"],
             region=True)
        S.op("dve", lambda e: e.scalar_tensor_tensor(out=bs2[:], in0=bs_r, scalar=cst[0:2, O_M01 + 1:O_M01 + 2],
                                                     in1=bs_f, op0=ALU.mult, op1=ALU.add),
             reads=["bs_f", "bs_r"], writes=["bs2"], region=True)

    def rn(g, *a):
        return g.name + "." + ".".join(str(x) for x in a)

    def cur_x(g):
        return (g.xcur, g.xn)

    def norm_sq(g, si, c0, c1, xb):
        x, xn = xb
        S.op("act", lambda e, g=g, c0=c0, c1=c1, x=x: e.activation(out=g.ycat[:, :, c0:c1], in_=x[:, :, c0:c1],
                                                                   func=AF.Square),
             reads=[rn(g, xn, k, si) for k in range(KC)], writes=[rn(g, "yc", k, si) for k in range(KC)])

    def norm_rest(g, si, c0, c1, xb, goff, inplace_y=False, sq="yc"):
        x, xn = xb
        n = c1 - c0
        b = next_bank()
        sqt = g.ycat if sq == "yc" else g.hid
        mm_group([(ps[:, b, 0:n], ones_b[:], sqt[:, k, c0:c1], k == 0, k == KC - 1) for k in range(KC)],
                 reads=[rn(g, sq, k, si) for k in range(KC)], writes=["ps%d" % b],
                 region=(sq == "hid" and g.region))
        S.op("act", lambda e, g=g, b=b, n=n, c0=c0, c1=c1: e.activation(out=g.rstd[:, c0:c1], in_=ps[:, b, 0:n],
                                                                        func=AF.Sqrt, bias=C(O_EPS), scale=1.0 / D),
             reads=["ps%d" % b], writes=[rn(g, "rs", si)])
        S.op("dve", lambda e, g=g, c0=c0, c1=c1: e.reciprocal(out=g.rstd[:, c0:c1], in_=g.rstd[:, c0:c1]),
             reads=[rn(g, "rs", si)], writes=[rn(g, "rs", si)])
        for k in range(KC):
            if inplace_y:
                o = x[:, k, c0:c1]
                wr = [rn(g, xn, k, si)]
            else:
                o = g.h[:, k, 2 + c0:2 + c1]
                wr = [rn(g, "h", k, si)]
            S.op("dve", lambda e, g=g, o=o, k=k, c0=c0, c1=c1, x=x: e.scalar_tensor_tensor(
                out=o, in0=x[:, k, c0:c1], scalar=C(goff + k), in1=g.rstd[:, c0:c1],
                op0=ALU.mult, op1=ALU.mult),
                reads=[rn(g, xn, k, si), rn(g, "rs", si)], writes=wr)

    def norm_units(us, xsel, goff, inplace_y=False):
        for (g, si, c0, c1) in us:
            norm_sq(g, si, c0, c1, xsel(g))
            norm_rest(g, si, c0, c1, xsel(g), goff, inplace_y)

    CU = [None]
    PENDING = []

    def flush_pending():
        while PENDING:
            PENDING.pop(0)()


    def units_of(ti, l, segs):
        us = []
        for g in segs:
            off = HOFF[l] if (TRIM and g is main and ti == 0) else 0
            for si, (c0, c1) in enumerate(g.subs):
                if c1 > off:
                    us.append((g, si, max(c0, off), c1))
        return us

    def out_proj(segs, l, wd, nk, srcname, src_of, next_norm, extra_after_a=None, region=False):
        us = CU[0]
        groups = [us[:1], us[1:]] if (STAGGER and len(us) > 1) else [us]
        if next_norm is not None:
            nus, nxsel, ngoff = next_norm
            ngroups = [nus[:1], nus[1:]] if len(groups) > 1 else [nus]
        presq = PRESQ and len(groups) > 1
        sqname = "yc" if nk == NJ else "hid"
        presq_cnt = {}
        a_presq = False
        nmap = {}
        if presq and next_norm is not None:
            nmap = {(g.name, si): (c0, c1) for (g, si, c0, c1) in ngroups[0] + ngroups[1]}
        for gi, ug in enumerate(groups):
            for m in range(KC):
                if nk == KC:
                    slot, w = wload(wd[l, m], nk * 128)
                    wv = w.rearrange("p (k m) -> p k m", k=nk)
                    wk = [wv[:, k, :] for k in range(nk)]
                    slots = [slot]
                else:
                    wk = []
                    slots = []
                    for hh in range(2):
                        slot, w = wload(wd[l, m][:, hh * HK * 128:(hh + 1) * HK * 128], HK * 128, keep=(hh > 0))
                        wv = w.rearrange("p (k m) -> p k m", k=HK)
                        wk += [wv[:, k, :] for k in range(HK)]
                        slots.append(slot)
                for (g, si, c0, c1) in ug:
                    n = c1 - c0
                    b = next_bank()
                    src = src_of(g)
                    mm_group([(ps[:, b, 0:n], wk[k], src[:, k, c0:c1], k == 0, k == nk - 1) for k in range(nk)],
                             reads=["ring%d" % sl for sl in slots] + [rn(g, srcname, k, si) for k in range(nk)],
                             writes=["ps%d" % b], region=(region and g.region))
                    S.op("dve", lambda e, g=g, m=m, b=b, n=n, c0=c0, c1=c1, x=g.xcur: e.tensor_tensor(
                        out=x[:, m, c0:c1], in0=ps[:, b, 0:n], in1=x[:, m, c0:c1], op=ALU.add),
                        reads=["ps%d" % b, rn(g, g.xn, m, si)], writes=[rn(g, g.xn, m, si)])
                    if presq and next_norm is not None and (g.name, si) in nmap:
                        x_, xn_ = nxsel(g)
                        if xn_ == g.xn:
                            c0n, c1n = nmap[(g.name, si)]
                            sqt = g.ycat if sqname == "yc" else g.hid
                            S.op("act", lambda e, sqt=sqt, m=m, c0n=c0n, c1n=c1n, x_=x_: e.activation(
                                out=sqt[:, m, c0n:c1n], in_=x_[:, m, c0n:c1n], func=AF.Square),
                                reads=[rn(g, xn_, m, si)], writes=[rn(g, sqname, m, si)],
                                region=(sqname == "hid" and g.region))
                            presq_cnt[(g.name, si)] = presq_cnt.get((g.name, si), 0) + 1
                if len(groups) > 1 and gi == 1 and m == (0 if a_presq else 1) and next_norm is not None:
                    for (g, si, c0, c1) in ngroups[0]:
                        norm_rest(g, si, c0, c1, nxsel(g), ngoff,
                                  sq=(sqname if presq_cnt.get((g.name, si), 0) == KC else "yc"))
            if gi == 0:
                if extra_after_a is not None:
                    extra_after_a()
                if next_norm is not None:
                    if len(groups) > 1:
                        a_presq = all(presq_cnt.get((g.name, si), 0) == KC for (g, si, c0, c1) in ngroups[0])
                        for (g, si, c0, c1) in ngroups[0]:
                            if presq_cnt.get((g.name, si), 0) != KC:
                                norm_sq(g, si, c0, c1, nxsel(g))
                    else:
                        norm_units(nus, nxsel, ngoff)
            elif next_norm is not None:
                sqs = {}
                for (g, si, c0, c1) in ngroups[1]:
                    if presq_cnt.get((g.name, si), 0) != KC:
                        norm_sq(g, si, c0, c1, nxsel(g))
                        sqs[(g.name, si)] = "yc"
                    else:
                        sqs[(g.name, si)] = sqname
                PENDING.append(lambda us=ngroups[1], nxsel=nxsel, ngoff=ngoff, sqs=sqs: [
                    norm_rest(g, si, c0, c1, nxsel(g), ngoff, sq=sqs[(g.name, si)]) for (g, si, c0, c1) in us])

    def mixer(segs, l, first_tile):
        for g in segs:
            S.op("act", lambda e, g=g: e.activation(out=g.P[:, :, 0:16], in_=g.phist[:, l, :, :], func=AF.Copy),
                 reads=[rn(g, "ph", l)], writes=[rn(g, "P", "hist")], region=g.region)
            S.op("act", lambda e, g=g: e.activation(out=g.Q[:, :, 0:2], in_=g.qhist[:, l, :, :], func=AF.Copy),
                 reads=[rn(g, "qh", l)], writes=[rn(g, "Q", "hist")], region=g.region)
        S.op("act", lambda e: e.activation(out=main.h[:, :, 0:2], in_=main.hhist[:, l, :, :], func=AF.Copy),
             reads=[rn(main, "hh", l)], writes=[rn(main, "h", "hist")])

        def hreads(g, si):
            return [rn(g, "h", k, si) for k in range(KC)]

        def fm_chunk(widx, consume):
            slot, w = wload(w_in_d[l, widx], KC * 128)
            wv = w.rearrange("p (k m) -> p k m", k=KC)
            for (g, si, c0, c1) in CU[0]:
                if True:
                    n = c1 - c0
                    b = next_bank()
                    mm_group([(ps[:, b, 0:n], wv[:, k, :], g.h[:, k, 2 + c0:2 + c1], k == 0, k == KC - 1)
                              for k in range(KC)],
                             reads=["ring%d" % slot] + hreads(g, si), writes=["ps%d" % b])
                    consume(g, si, c0, c1, b, n)

        us_all = CU[0]
        ugroups = [us_all[:1], us_all[1:]] if (STAGGER and len(us_all) > 1) else [us_all]
        fw = []
        for i_, widx in enumerate((8, 9, 10, 11, 4, 5)):
            slot, w = wload(w_in_d[l, widx], KC * 128, keep=(i_ > 0))
            fw.append((slot, w.rearrange("p (k m) -> p k m", k=KC)))
        flush_pending()
        for gi_, ug in enumerate(ugroups):
            for i_ in range(6):
                slot, wv = fw[i_]
                j = i_ % 2
                for (g, si, c0, c1) in ug:
                    n = c1 - c0
                    b = next_bank()
                    mm_group([(ps[:, b, 0:n], wv[:, k, :], g.h[:, k, 2 + c0:2 + c1], k == 0, k == KC - 1)
                              for k in range(KC)],
                             reads=["ring%d" % slot] + hreads(g, si), writes=["ps%d" % b])
                    if i_ < 2:
                        S.op("act", lambda e, g=g, j=j, c0=c0, c1=c1, b=b, n=n: e.activation(
                            out=g.acc[:, j, c0:c1], in_=ps[:, b, 0:n], func=AF.Copy),
                            reads=["ps%d" % b], writes=[rn(g, "acc", j, si)], region=g.region)
                    elif i_ >= 4:
                        S.op("act", lambda e, g=g, j=j, c0=c0, c1=c1, b=b, n=n: e.activation(
                            out=g.P[:, j, 16 + c0:16 + c1], in_=ps[:, b, 0:n], func=AF.Copy),
                            reads=["ps%d" % b], writes=[rn(g, "P", j, si)], region=g.region)
                    else:
                        S.op("dve", lambda e, g=g, j=j, c0=c0, c1=c1, b=b, n=n: e.tensor_tensor(
                            out=g.Q[:, j, 2 + c0:2 + c1], in0=ps[:, b, 0:n], in1=g.acc[:, j, c0:c1], op=ALU.mult),
                            reads=["ps%d" % b, rn(g, "acc", j, si)], writes=[rn(g, "Q", j, si)], region=g.region)
                if gi_ == 0 and i_ == 1:
                    flush_pending()
        flush_pending()
        wrelease()
        for g in segs:
            ns = len(g.subs)
            for j in range(2):
                allQ = [rn(g, "Q", "hist")] + [rn(g, "Q", j, si) for si in range(ns)]
                wc = O_WCV + (l * 2 + j) * 3
                accw = [rn(g, "acc", j, si) for si in range(ns)]
                S.op("act", lambda e, g=g, j=j, wc=wc, T_=g.T: e.activation(out=g.acc[:, j, 0:T_], in_=g.Q[:, j, 2:2 + T_],
                                                                    func=AF.Identity, scale=C(wc + 2)),
                     reads=allQ, writes=accw, region=g.region)
                S.op("dve", lambda e, g=g, j=j, wc=wc, T_=g.T: e.scalar_tensor_tensor(
                    out=g.acc[:, j, 0:T_], in0=g.Q[:, j, 1:1 + T_], scalar=C(wc + 1), in1=g.acc[:, j, 0:T_],
                    op0=ALU.mult, op1=ALU.add), reads=allQ + accw, writes=accw, region=g.region)
                S.op("dve", lambda e, g=g, j=j, wc=wc, T_=g.T: e.scalar_tensor_tensor(
                    out=g.acc[:, j, 0:T_], in0=g.Q[:, j, 0:T_], scalar=C(wc), in1=g.acc[:, j, 0:T_],
                    op0=ALU.mult, op1=ALU.add), reads=allQ + accw, writes=accw, region=g.region)
            allQ2 = [rn(g, "Q", "hist")] + [rn(g, "Q", j, si) for j in range(2) for si in range(ns)]
            S.op("act", lambda e, g=g, T_=g.T: e.activation(out=g.qhist[:, l, :, :], in_=g.Q[:, :, T_:T_ + 2], func=AF.Copy),
                 reads=allQ2, writes=[rn(g, "qh", l)], region=g.region)
        for g in segs:
            W = 16 + g.T
            allP = [rn(g, "P", "hist")] + [rn(g, "P", j, si) for j in range(2) for si in range(len(g.subs))]
            S.op(CHAIN_ENG, lambda e, g=g, W=W: e.tensor_tensor(out=g.A1[:, :, 1:W], in0=g.P[:, :, 1:W],
                                                            in1=g.P[:, :, 0:W - 1], op=ALU.add),
                 reads=allP, writes=[rn(g, "A1", 0), rn(g, "A1", 1)], region=g.region)
            S.op(CHAIN_ENG, lambda e, g=g, W=W: e.tensor_tensor(out=g.A2[:, :, 3:W], in0=g.A1[:, :, 3:W],
                                                            in1=g.A1[:, :, 1:W - 2], op=ALU.add),
                 reads=[rn(g, "A1", 0), rn(g, "A1", 1)], writes=[rn(g, "A2", 0), rn(g, "A2", 1)], region=g.region)
            S.op(CHAIN_ENG, lambda e, g=g, W=W: e.tensor_tensor(out=g.A1[:, 1, 7:W], in0=g.A2[:, 1, 7:W],
                                                            in1=g.A2[:, 1, 3:W - 4], op=ALU.add),
                 reads=[rn(g, "A2", 1)], writes=[rn(g, "A1", 1)], region=g.region)
            S.op(CHAIN_ENG, lambda e, g=g, W=W: e.tensor_tensor(out=g.A2[:, 1, 15:W], in0=g.A1[:, 1, 15:W],
                                                            in1=g.A1[:, 1, 7:W - 8], op=ALU.add),
                 reads=[rn(g, "A1", 1)], writes=[rn(g, "A2", 1)], region=g.region)
            S.op("act", lambda e, g=g, T_=g.T: e.activation(out=g.phist[:, l, :, :], in_=g.P[:, :, T_:T_ + 16],
                                                    func=AF.Copy),
                 reads=allP, writes=[rn(g, "ph", l)], region=g.region)
        def pool_d(g):
            allP = [rn(g, "P", "hist")] + [rn(g, "P", j, si) for j in range(2) for si in range(len(g.subs))]
            for j in range(2):
                for half in range(2):
                    src = g.A1 if half == 0 else g.A2
                    p0, p1 = half * 64, half * 64 + 64
                    S.op("dve", lambda e, g=g, j=j, src=src, p0=p0, p1=p1, T_=g.T: e.scalar_tensor_tensor(
                        out=g.d[p0:p1, j, 0:T_], in0=src[p0:p1, j, 16:16 + T_], scalar=cst[p0:p1, O_INVW + j:O_INVW + j + 1],
                        in1=g.P[p0:p1, j, 16:16 + T_], op0=ALU.mult, op1=ALU.subtract),
                        reads=allP + [rn(g, "A1", j), rn(g, "A2", j)], writes=[rn(g, "d", j, half)], region=g.region)
                    if first_tile and g is main:
                        a0 = 16 + HALO
                        S.op("dve", lambda e, g=g, j=j, src=src, p0=p0, p1=p1, a0=a0: e.tensor_tensor(
                            out=src[p0:p1, j, a0:a0 + 16], in0=src[p0:p1, j, a0:a0 + 16],
                            in1=cst[p0:p1, O_INVC + 16 * j:O_INVC + 16 * j + 16], op=ALU.mult),
                            reads=[rn(g, "d", j, half)], writes=[rn(g, "A1", j), rn(g, "A2", j)], region=g.region)
                        S.op("dve", lambda e, g=g, j=j, src=src, p0=p0, p1=p1, a0=a0: e.tensor_tensor(
                            out=g.d[p0:p1, j, HALO:HALO + 16], in0=src[p0:p1, j, a0:a0 + 16],
                            in1=g.P[p0:p1, j, a0:a0 + 16], op=ALU.subtract),
                            reads=allP + [rn(g, "A1", j), rn(g, "A2", j)], writes=[rn(g, "d", j, half)],
                            region=g.region)
        def pool_mm(g, j):
                for (g_, si, c0, c1) in CU[0]:
                    if g_ is not g:
                        continue
                    n = c1 - c0
                    b = next_bank()
                    mm_group([(ps[:, b, 0:n], wpool[:, l * 2 + j, :], g.d[:, j, c0:c1], True, True)],
                             reads=[rn(g, "d", j, 0), rn(g, "d", j, 1)], writes=["ps%d" % b], region=g.region)
                    S.op("act", lambda e, g=g, j=j, b=b, n=n, c0=c0, c1=c1: e.activation(
                        out=g.ycat[:, 4 + j, c0:c1], in_=ps[:, b, 0:n], func=AF.Identity,
                        scale=C(O_PSC + l * 2 + j)),
                        reads=["ps%d" % b], writes=[rn(g, "yc", 4 + j, si)])
        vslots = []
        for hf in range(2):
            slot, w = wload(w_v_d[l, hf], KC * 256, keep=(hf > 0))
            vslots.append((slot, w.rearrange("p (k m) -> p k m", k=KC)))
        for g in segs:
            g.chunks = [(t0, min(128, g.T - t0)) for t0 in range(0, g.T, 128) if t0 >= g.off]
            for (t0, L) in g.chunks:
                ci = t0 // 128
                b = next_bank()
                sis = sorted(set(si for si, (c0, c1) in enumerate(g.subs) if c0 < t0 + L and c1 > t0))
                rd = []
                for si in sis:
                    rd += hreads(g, si)
                mms = []
                for hf in range(2):
                    for k in range(KC):
                        mms.append((ps[0:L, b, hf * 256:(hf + 1) * 256], g.h[:, k, 2 + t0:2 + t0 + L],
                                    vslots[hf][1][:, k, :], k == 0, k == KC - 1))
                mm_group(mms, reads=["ring%d" % vslots[0][0], "ring%d" % vslots[1][0]] + rd, writes=["ps%d" % b])
                S.op("act", lambda e, g=g, ci=ci, L=L, b=b: e.activation(out=g.vt[0:L, ci, :], in_=ps[0:L, b, :],
                                                                         func=AF.Copy),
                     reads=["ps%d" % b], writes=[rn(g, "vt", ci)], region=g.region)
                if g.vf is not None:
                    S.op("act", lambda e, g=g, L=L, b=b: e.activation(out=g.vf[0:L, :], in_=ps[0:L, b, :],
                                                                      func=AF.Copy),
                         reads=["ps%d" % b], writes=[rn(g, "vf")])
                    S.op("sp", lambda e, g=g: e.dma_start(out=o_v[:, l, :], in_=g.vf[:]),
                         reads=[rn(g, "vf")], writes=["vst"], sem="vst", inc=16)
        for m in range(4):
            def cons_u(g, si, c0, c1, b, n, m=m):
                S.op("act", lambda e: e.activation(out=g.u[:, m, c0:c1], in_=ps[:, b, 0:n], func=AF.Copy),
                     reads=["ps%d" % b], writes=[rn(g, "u", m, si)], region=g.region)
            fm_chunk(m, cons_u)
        for g in segs:
            pool_d(g)
        for g in segs:
            for hd in range(4):
                for grp in range(0, len(g.chunks), 4):
                    cs = g.chunks[grp:grp + 4]
                    b = next_bank()
                    mms = []
                    rd = ["wsT", "bs2"]
                    tot = 0
                    Lc = cs[0][1]
                    nch = len(cs)
                    bo = (l * 4 + hd) * 128
                    brhs = bs2[0:2, bo:bo + Lc]
                    if nch > 1:
                        brhs = bs2v[0:2, l * 4 + hd:l * 4 + hd + 1, 0:Lc].broadcast_to([2, nch, Lc])
                        bout = ps[:, b, 0:nch * Lc].rearrange("p (c q) -> p c q", c=nch)
                    else:
                        bout = ps[:, b, 0:Lc]
                    mms.append((bout, ones_b[0:2, :], brhs, True, False))
                    for i_c, (t0, L) in enumerate(cs):
                        ci = t0 // 128
                        o = ps[:, b, tot:tot + L]
                        mms.append((o, g.vt[0:L, ci, hd * 128:(hd + 1) * 128], wsT[0:L, l * 4 + hd, 0:L], False,
                                    i_c == nch - 1))
                        rd.append(rn(g, "vt", ci))
                        tot += L
                    mm_group(mms, reads=rd, writes=["ps%d" % b], region=g.region)
                    q0 = cs[0][0]
                    sis = sorted(set(si for si, (c0, c1) in enumerate(g.subs) if c0 < q0 + tot and c1 > q0))
                    S.op("dve", lambda e, g=g, hd=hd, b=b, q0=q0, tot=tot: e.tensor_tensor(
                        out=g.ycat[:, hd, q0:q0 + tot], in0=ps[:, b, 0:tot], in1=g.u[:, hd, q0:q0 + tot], op=ALU.mult),
                        reads=["ps%d" % b] + [rn(g, "u", hd, si) for si in sis],
                        writes=[rn(g, "yc", hd, si) for si in sis], region=g.region)
        for g in segs:
            for j in range(2):
                pool_mm(g, j)
        for j in range(2):
            def cons_gb(g, si, c0, c1, b, n, j=j):
                S.op("dve", lambda e: e.tensor_tensor(out=g.ycat[:, 6 + j, c0:c1], in0=ps[:, b, 0:n],
                                                      in1=g.acc[:, j, c0:c1], op=ALU.mult),
                     reads=["ps%d" % b] + [rn(g, "acc", j, s2) for s2 in range(len(g.subs))],
                     writes=[rn(g, "yc", 6 + j, si)], region=g.region)
            fm_chunk(6 + j, cons_gb)
        out_proj(segs, l, w_out_d, KC, "yc", lambda g: g.ycat, (CU[0], cur_x, O_G2 + l * KC))

    def ffn(segs, l, first_tile, save_state, next_norm):
        for g in segs:
            ns = len(g.subs)
            pass

        def hh_save():
            g = main
            ns = len(g.subs)
            S.op("act", lambda e, g=g, T_=g.T: e.activation(out=g.hhist[:, l, :, :], in_=g.h[:, :, T_:T_ + 2],
                                                    func=AF.Copy),
                 reads=[rn(g, "h", k, ns - 1) for k in range(KC)], writes=[rn(g, "hh", l)])
        us_all = CU[0]
        NFRONT = 4 if (STAGGER and len(us_all) > 1) else 0
        plan = []
        for ug in ([us_all[:1], us_all[1:]] if NFRONT else []):
            for j in range(NFRONT):
                for u_ in ug:
                    plan.append((j, u_))
        NBACK = 2 if NFRONT else 0
        for j in range(NFRONT, NJ - NBACK):
            for u_ in us_all:
                plan.append((j, u_))
        for ug in ([us_all[:1], us_all[1:]] if NBACK else []):
            for j in range(NJ - NBACK, NJ):
                for u_ in ug:
                    plan.append((j, u_))
        upw = {}
        flush_pending()
        hh_save()
        for pi_, (j, (g, si, c0, c1)) in enumerate(plan):
            if j not in upw:
                if j < NFRONT:
                    blk = list(range(NFRONT))
                elif NBACK and j >= NJ - NBACK:
                    blk = list(range(NJ - NBACK, NJ))
                else:
                    blk = [j]
                for jj in blk:
                    slot, w = wload(w_up_d[l, jj], KC * 256, keep=(jj != blk[0]))
                    upw[jj] = (slot, w.rearrange("p (k t m) -> p k t m", k=KC, t=2))
            slot, wv = upw[j]
            if True:
                ns = len(g.subs)
                if True:
                    n = c1 - c0
                    hr = [rn(g, "h", k, si) for k in range(KC)]
                    if g is main:
                        hr += [rn(g, "h", k, si - 1) for k in range(KC)] if si > 0 else [rn(g, "h", "hist")]
                    banks = []
                    for t in range(2):
                        b = next_bank()
                        banks.append(b)
                        c = t * NJ + j
                        mms = []
                        if g is main:
                            mms += [(ps[:, b, 0:n + 2], wv[:, k, t, :], g.h[:, k, c0:c1 + 2], k == 0, k == KC - 1)
                                    for k in range(KC)]
                        else:
                            mms.append((ps[:, b, 0:2], ident[:], g.upst[:, l, c, :], True, True))
                            mms += [(ps[:, b, 2:n + 2], wv[:, k, t, :], g.h[:, k, 2 + c0:2 + c1], k == 0, k == KC - 1)
                                    for k in range(KC)]
                        mm_group(mms, reads=["ring%d" % slot] + hr + ([rn(g, "ups", l, c)] if g is not main else []),
                                 writes=["ps%d" % b])
                    fi = fs_ctr[0] % NF
                    fs_ctr[0] += 1
                    bufs = [fscr[t][fi][:, 0:n] for t in range(2)]
                    sbuf_ = fscr[2][fi][:, 0:n]
                    for t in range(2):
                        b = banks[t]
                        c = t * NJ + j
                        wo = O_WF + (l * NCH + c) * 3
                        bo = O_BF + l * NCH + c
                        buf = bufs[t]
                        rname = "f%d.%d" % (t, fi)
                        if save_state and si == ns - 1:
                            S.op("act", lambda e, g=g, b=b, n=n, c=c: e.activation(out=g.upst[:, l, c, :],
                                                                                  in_=ps[:, b, n:n + 2], func=AF.Copy),
                                 reads=["ps%d" % b], writes=[rn(g, "ups", l, c)])
                        S.op("act", lambda e, b=b, n=n, buf=buf, wo=wo, bo=bo: e.activation(
                            out=buf, in_=ps[:, b, 2:n + 2], func=AF.Identity, bias=C(bo), scale=C(wo + 2)),
                            reads=["ps%d" % b], writes=[rname], region=True)
                    for t in range(2):
                        b = banks[t]
                        c = t * NJ + j
                        wo = O_WF + (l * NCH + c) * 3
                        buf = bufs[t]
                        rname = "f%d.%d" % (t, fi)
                        S.op("dve", lambda e, b=b, n=n, buf=buf, wo=wo: e.scalar_tensor_tensor(
                            out=buf, in0=ps[:, b, 1:n + 1], scalar=C(wo + 1), in1=buf, op0=ALU.mult, op1=ALU.add),
                            reads=["ps%d" % b, rname], writes=[rname], region=True)
                        S.op("dve", lambda e, b=b, n=n, buf=buf, wo=wo: e.scalar_tensor_tensor(
                            out=buf, in0=ps[:, b, 0:n], scalar=C(wo), in1=buf, op0=ALU.mult, op1=ALU.add),
                            reads=["ps%d" % b, rname], writes=[rname], region=True)
                        if t == 0:
                            S.op("act", lambda e, buf=buf, sbuf_=sbuf_: e.activation(out=sbuf_, in_=buf, func=AF.Silu),
                                 reads=[rname], writes=["f2.%d" % fi], region=True)
                    S.op(MUL_ENG, lambda e, g=g, j=j, c0=c0, c1=c1, sbuf_=sbuf_, abuf=bufs[1]: e.tensor_tensor(
                        out=g.hid[:, j, c0:c1], in0=sbuf_, in1=abuf, op=ALU.mult),
                        reads=["f2.%d" % fi, "f1.%d" % fi], writes=[rn(g, "hid", j, si)], region=g.region)
        def halo_flag():
            if first_tile and l < DEPTH - 1:
                g = main
                assert g.subs[0][1] >= HALO
                xs = [rn(g, g.xn, k, 0) for k in range(KC)]
                S.op("dve", lambda e, g=g, x=g.xcur: e.tensor_scalar(out=x[:, :, 0:HALO], in0=x[:, :, 0:HALO],
                                                                   scalar1=C(O_FLAG), scalar2=None, op0=ALU.mult),
                     reads=xs, writes=xs)

        out_proj(segs, l, w_dn_d, NJ, "hid", lambda g: g.hid, next_norm, extra_after_a=halo_flag, region=True)

    st_ctr = [0]

    def store(src_ap, dst_ap, reads, region=False):
        i = st_ctr[0] % 4
        st_ctr[0] += 1
        S.op("sp", lambda e, s=src_ap, d=dst_ap: e.dma_start(out=d, in_=s), reads=reads, writes=["stq%d" % i],
             sem="st%d" % i, inc=16, region=region)

    init()
    tok0s = [sum(tiles[:i]) for i in range(NT)]

    def load_x(ti):
        par = ti % 2
        T_ = tiles[ti]
        xw = ["m.x%d.%d.%d" % (par, k, si) for k in range(KC) for si in range(len(subtiles(T_, main.maxn)))]
        S.op("sp", lambda e, T_=T_, t0=tok0s[ti], x=main.xb[par]: e.dma_start(out=x[:, :, 0:T_],
                                                                               in_=xT[:, :, t0:t0 + T_]),
             writes=xw, sem="xld%d" % par, inc=16)

    load_x(0)
    for ti, T in enumerate(tiles):
        tok0 = tok0s[ti]
        main.T = T
        main.xcur = main.xb[ti % 2]
        main.xn = "x%d" % (ti % 2)
        first = ti == 0
        last = ti == NT - 1
        segs = [main] + ([samp] if (samp is not None and last) else [])
        main.subs = subtiles(T, main.maxn)
        if samp is not None:
            samp.subs = subtiles(SAMP, SAMP)
        U = [units_of(ti, l, segs) for l in range(DEPTH)]
        if first:
            norm_units([u_ for u_ in U[0] if u_[0] is main], cur_x, O_G1)
        norm_units([u_ for u_ in U[0] if u_[0] is not main], cur_x, O_G1)
        for l in range(DEPTH):
            CU[0] = U[l]
            main.off = HOFF[l] if (TRIM and first) else 0
            S.phase_switch()
            mixer(segs, l, first)
            S.phase_switch()
            if l < DEPTH - 1:
                nn = (U[l + 1], cur_x, O_G1 + (l + 1) * KC)
            elif not last:
                load_x(ti + 1)
                npar = (ti + 1) % 2
                nsubs = subtiles(tiles[ti + 1], main.maxn)
                nus = [(main, si, c0, c1) for si, (c0, c1) in enumerate(nsubs)]
                nn = (nus, (lambda g, npar=npar: (g.xb[npar], "x%d" % npar)), O_G1)
            else:
                nn = None
            ffn(segs, l, first, save_state=last, next_norm=nn)
        S.phase_switch()
        flush_pending()
        for (g, si, c0, c1) in U[DEPTH - 1]:
            norm_sq(g, si, c0, c1, cur_x(g))
            norm_rest(g, si, c0, c1, cur_x(g), O_FG, inplace_y=True)
            yr = [rn(g, g.xn, k, si) for k in range(KC)]
            if g is main:
                a = max(c0, HALO if first else 0)
                if a < c1:
                    o0 = tok0 + a - HALO
                    store(g.xcur[:, :, a:c1], yT_d[:, :, o0:o0 + (c1 - a)], yr)
            else:
                store(g.xcur[:, :, :], ysT_d, yr)
    for g in ([main] + ([samp] if samp else [])):
        nm = g.name
        S.op("sp", lambda e, g=g, nm=nm: e.dma_start(out=o_pool[nm], in_=g.phist[:]),
             reads=[rn(g, "ph", l) for l in range(DEPTH)], sem="misc", inc=16)
        S.op("sp", lambda e, g=g, nm=nm: e.dma_start(out=o_conv[nm], in_=g.qhist[:]),
             reads=[rn(g, "qh", l) for l in range(DEPTH)], sem="misc", inc=16)
        S.op("sp", lambda e, g=g, nm=nm: e.dma_start(out=o_ffn[nm], in_=g.upst[:]),
             reads=[rn(g, "ups", l, c) for l in range(DEPTH) for c in range(NCH)], sem="misc", inc=16)
    S.wait_all("sp", ["st0", "st1", "st2", "st3", "misc", "vst"])

    from contextlib import ExitStack
    with ExitStack() as es:
        sems = {n: es.enter_context(nc.semaphore(n)) for n in sem_names}
        block = es.enter_context(nc.Block())

        def replay(engobj, prog):
            for it in prog:
                if it[0] == "wait":
                    engobj.wait_ge(sems[it[1]], it[2])
                else:
                    ins = it[1](engobj)
                    ins.then_inc(sems[it[2]], it[3])

        @block.sync
        def _(e):
            replay(e, S.progs["sp"])

        @block.gpsimd
        def _(e):
            replay(e, S.progs["pool"])

        @block.tensor
        def _(e):
            replay(e, S.progs["pe"])

        @block.scalar
        def _(e):
            replay(e, S.progs["act"])

        @block.vector
        def _(e):
            replay(e, S.progs["dve"])
    return nc


def _fm(a):
    a = np.asarray(a, np.float32)
    n = a.shape[-1] // 128
    a = a.reshape(a.shape[:-1] + (n, 128))
    return np.ascontiguousarray(np.moveaxis(a, -1, 0))


def prep_shared(inp):
    f = lambda k: np.asarray(inp[k], np.float32)
    w_in, w_out, w_up, w_down = f("w_in"), f("w_out"), f("w_up"), f("w_down")
    sh = {}
    cols = [0, 1, 2, 3, 8, 9, 10, 11, 12, 13, 14, 15]
    wi = w_in.reshape(DEPTH, KC, 128, 16, 128)
    sh["w_in_r"] = np.ascontiguousarray(wi[:, :, :, cols, :].transpose(0, 3, 2, 1, 4)).reshape(DEPTH, 12, 128, KC * 128)
    wv = w_in[:, :, 512:1024].reshape(DEPTH, KC, 128, 2, 256)
    sh["w_v_r"] = np.ascontiguousarray(wv.transpose(0, 3, 2, 1, 4)).reshape(DEPTH, 2, 128, KC * 256)
    wo = w_out.reshape(DEPTH, KC, 128, 8, 128)
    sh["w_out_r"] = np.ascontiguousarray(wo.transpose(0, 3, 2, 1, 4)).reshape(DEPTH, 8, 128, KC * 128)
    wu = w_up.reshape(DEPTH, KC, 128, 2, NJ, 128)
    sh["w_up_r"] = np.ascontiguousarray(wu.transpose(0, 4, 2, 1, 3, 5)).reshape(DEPTH, NJ, 128, KC * 256)
    wd = w_down.reshape(DEPTH, NJ, 128, 8, 128)
    sh["w_dn_r"] = np.ascontiguousarray(wd.transpose(0, 3, 2, 1, 4)).reshape(DEPTH, 8, 128, NJ * 128)
    sh["wsT"] = np.ascontiguousarray(f("w_s").transpose(3, 0, 1, 2)).reshape(128, DEPTH * 4, 128)
    k = np.arange(128)
    sh["mask"] = (k[:, None] <= k[None, :]).astype(np.float32)
    bs = f("b_s").reshape(1, DEPTH * 4 * 128)
    sh["bs2"] = np.ascontiguousarray(np.concatenate([bs, bs], 0))
    wp = f("w_pool")
    wpb = np.zeros((128, DEPTH, 2, 128), np.float32)
    for j in range(2):
        for half in range(2):
            wpb[half * 64:(half + 1) * 64, :, j, half * 64:(half + 1) * 64] = wp[:, 2 * j + half].transpose(1, 0, 2)
    sh["wpool"] = wpb.reshape(128, DEPTH * 2, 128)
    sh["ident"] = np.eye(128, dtype=np.float32)
    return sh


def prep_cst(inp, seq_start):
    f = lambda k: np.asarray(inp[k], np.float32)
    c = np.zeros((128, NCST), np.float32)
    c[:, O_G1:O_G1 + DEPTH * KC] = _fm(f("norm1_g")).reshape(128, -1)
    c[:, O_G2:O_G2 + DEPTH * KC] = _fm(f("norm2_g")).reshape(128, -1)
    c[:, O_FG:O_FG + KC] = _fm(f("final_g")).reshape(128, -1)
    c[:, O_PSC:O_PSC + DEPTH * 2] = _fm(f("pool_scale")).reshape(128, -1)
    c[:, O_WCV:O_WCV + DEPTH * 6] = _fm(f("w_conv")).transpose(0, 1, 3, 2).reshape(128, -1)
    c[:, O_WF:O_WF + DEPTH * NCH * 3] = _fm(f("w_fconv")).transpose(0, 1, 3, 2).reshape(128, -1)
    c[:, O_BF:O_BF + DEPTH * NCH] = _fm(f("b_fconv")).reshape(128, -1)
    win = np.array([[2, 4], [8, 16]], np.float32)
    t = np.arange(16, dtype=np.float32)
    for j in range(2):
        for half in range(2):
            sl = slice(half * 64, half * 64 + 64)
            c[sl, O_INVW + j] = 1.0 / win[j, half]
            cnt = np.minimum(t + 1, win[j, half]) if seq_start else np.full(16, win[j, half], np.float32)
            c[sl, O_INVC + 16 * j:O_INVC + 16 * j + 16] = 1.0 / cnt
    c[:, O_FLAG] = 0.0 if seq_start else 1.0
    c[:, O_EPS] = EPS
    c[0, O_M01] = 1.0
    c[1, O_M01 + 1] = 1.0
    return c


_CACHE = {}


def run(inp, main_tok, tiles, nseg):
    key = (main_tok, tuple(tiles))
    if key not in _CACHE:
        _CACHE[key] = build(main_tok, tiles)
    nc = _CACHE[key]
    sh = prep_shared(inp)
    xp = np.asarray(inp["x_prompt"], np.float32)
    xs = np.asarray(inp["x_sample"], np.float32)
    sp = np.asarray(inp["state_pool"], np.float32)
    sc = np.asarray(inp["state_conv"], np.float32)
    sf = np.asarray(inp["state_ffn_conv"], np.float32)
    in_maps = []
    for c in range(8):
        b, i = c // nseg, c % nseg
        t0 = i * main_tok
        xt = np.zeros((HALO + main_tok, D), np.float32)
        if i == 0:
            xt[HALO:] = xp[b, 0:main_tok]
        else:
            xt[:] = xp[b, t0 - HALO:t0 + main_tok]
        m = dict(sh)
        m["xT"] = np.ascontiguousarray(xt.reshape(-1, KC, 128).transpose(2, 1, 0))
        m["xsT"] = np.ascontiguousarray(xs[c].reshape(SAMP, KC, 128).transpose(2, 1, 0))
        m["cst"] = prep_cst(inp, i == 0)
        spool = np.zeros((128, DEPTH, 2, 16), np.float32)
        spool[:, :, :, 1:] = sp[:, c].reshape(DEPTH, 15, 2, 128).transpose(3, 0, 2, 1)
        m["spool"] = spool
        m["sconv"] = np.ascontiguousarray(sc[:, c].reshape(DEPTH, 2, 2, 128).transpose(3, 0, 2, 1))
        m["sffn"] = np.ascontiguousarray(sf[:, c].reshape(DEPTH, 2, NCH, 128).transpose(3, 0, 2, 1))
        in_maps.append(m)
    res = run_bass_kernel_spmd(nc, in_maps, core_ids=list(range(8)))
    R = res.results
    nb = 8 // nseg
    y_prompt = np.empty((nb, nseg * main_tok, D), np.float32)
    for c in range(8):
        b, i = c // nseg, c % nseg
        y_prompt[b, i * main_tok:(i + 1) * main_tok] = R[c]["yT"].transpose(2, 1, 0).reshape(main_tok, D)
    y_sample = np.stack([R[c]["ysT"].transpose(2, 1, 0).reshape(SAMP, D) for c in range(8)])

    def pool_out(a):
        return a[:, :, :, 1:].transpose(1, 3, 2, 0).reshape(DEPTH, 15, 256)

    def conv_out(a):
        return a.transpose(1, 3, 2, 0).reshape(DEPTH, 2, 256)

    def ffn_out(a):
        return a.transpose(1, 3, 2, 0).reshape(DEPTH, 2, 2 * DFF)

    lastc = [b * nseg + nseg - 1 for b in range(nb)]
    new_pool_p = np.stack([pool_out(R[c]["pool_p"]) for c in lastc], 1)
    new_conv_p = np.stack([conv_out(R[c]["conv_p"]) for c in lastc], 1)
    new_ffn_p = np.stack([ffn_out(R[c]["ffn_p"]) for c in lastc], 1)
    new_pool_s = np.stack([pool_out(R[c]["pool_s"]) for c in range(8)], 1)
    new_conv_s = np.stack([conv_out(R[c]["conv_s"]) for c in range(8)], 1)
    new_ffn_s = np.stack([ffn_out(R[c]["ffn_s"]) for c in range(8)], 1)
    new_v_s = np.stack([R[c]["v_s"].transpose(1, 0, 2) for c in range(8)], 1)
    outs = (y_prompt, y_sample, new_pool_p, new_conv_p, new_ffn_p, new_pool_s, new_conv_s, new_ffn_s, new_v_s)
    return tuple(np.ascontiguousarray(o, dtype=np.float32) for o in outs)


def kernel(**inputs):
    return run(inputs, 4096, [896] * 5, 4)
```

```python
import numpy as np
import concourse.bass as bass
import concourse.mybir as mybir
from concourse.bass_utils import run_bass_kernel_spmd

F32 = mybir.dt.float32
BF16 = mybir.dt.bfloat16
ALU = mybir.AluOpType
AF = mybir.ActivationFunctionType

D = 1024
KC = 8
DEPTH = 4
DFF = 2816
NJ = 22
NCH = 44
HALO = 384
NSLOT = 8
PF = 7
STAGGER = True
PRESQ = True
TRIM = True
HOFF = (0, 128, 128, 256)
MUL_ENG = "pool"
CHAIN_ENG = "pool"
SLOT = KC * 256
HK = NJ // 2
EPS = 1e-6
SAMP = 32

O_G1 = 0
O_G2 = O_G1 + DEPTH * KC
O_FG = O_G2 + DEPTH * KC
O_PSC = O_FG + KC
O_WCV = O_PSC + DEPTH * 2
O_WF = O_WCV + DEPTH * 2 * 3
O_BF = O_WF + DEPTH * NCH * 3
O_INVW = O_BF + DEPTH * NCH
O_INVC = O_INVW + 2
O_FLAG = O_INVC + 32
O_EPS = O_FLAG + 1
O_M01 = O_EPS + 1
NCST = O_M01 + 2


class Sched:
    def __init__(self):
        self.progs = {e: [] for e in ("pe", "act", "dve", "pool", "sp")}
        self.count = {}
        self.waited = {e: {} for e in self.progs}
        self.res_w = {}
        self.res_r = {}
        self.cur_phase = {}
        self.prev_phase = {}

    def add_sem(self, name):
        self.count[name] = 0

    def phase_switch(self):
        for s, v in self.cur_phase.items():
            if self.prev_phase.get(s, 0) < v:
                self.prev_phase[s] = v
        self.cur_phase = {}

    def op(self, eng, fn, reads=(), writes=(), sem=None, inc=1, region=False):
        deps = {}

        def add(tok):
            if tok is None:
                return
            s, v = tok
            if deps.get(s, 0) < v:
                deps[s] = v

        for r in reads:
            add(self.res_w.get(r))
        for w in writes:
            add(self.res_w.get(w))
            for s, v in self.res_r.get(w, {}).items():
                add((s, v))
        if region:
            for s, v in self.prev_phase.items():
                add((s, v))
        prog = self.progs[eng]
        waited = self.waited[eng]
        for s, v in deps.items():
            if waited.get(s, 0) < v:
                waited[s] = v
                prog.append(("wait", s, v))
        semname = sem or eng
        self.count[semname] += inc
        tok = (semname, self.count[semname])
        prog.append(("op", fn, semname, inc))
        for r in reads:
            d = self.res_r.setdefault(r, {})
            if d.get(tok[0], 0) < tok[1]:
                d[tok[0]] = tok[1]
        for w in writes:
            self.res_w[w] = tok
            self.res_r[w] = {}
        if region:
            if self.cur_phase.get(tok[0], 0) < tok[1]:
                self.cur_phase[tok[0]] = tok[1]
        return tok

    def wait_all(self, eng, semnames):
        prog = self.progs[eng]
        for s in semnames:
            v = self.count[s]
            if v > 0 and self.waited[eng].get(s, 0) < v:
                self.waited[eng][s] = v
                prog.append(("wait", s, v))


class Seg:
    pass


def subtiles(T, maxn=448):
    out = []
    c = 0
    while c < T:
        n = min(maxn, T - c)
        out.append((c, c + n))
        c += n
    return out


def build(main_tok, tiles, with_sample=True):
    NTOK = HALO + main_tok
    assert sum(tiles) == NTOK and all(t % 128 == 0 for t in tiles)
    TMAX = max(tiles)
    NT = len(tiles)
    nc = bass.Bass("TRN2", target_bir_lowering=False)
    S = Sched()

    def din(name, shape):
        return nc.dram_tensor(name, list(shape), F32, kind="ExternalInput").ap()

    def dout(name, shape):
        return nc.dram_tensor(name, list(shape), F32, kind="ExternalOutput").ap()

    xT = din("xT", [128, KC, NTOK])
    xsT = din("xsT", [128, KC, SAMP])
    cst_d = din("cst", [128, NCST])
    wsT_d = din("wsT", [128, DEPTH * 4, 128])
    mask_d = din("mask", [128, 128])
    bs2_d = din("bs2", [2, DEPTH * 4 * 128])
    wpool_d = din("wpool", [128, DEPTH * 2, 128])
    ident_d = din("ident", [128, 128])
    spool_d = din("spool", [128, DEPTH, 2, 16])
    sconv_d = din("sconv", [128, DEPTH, 2, 2])
    sffn_d = din("sffn", [128, DEPTH, NCH, 2])
    w_in_d = din("w_in_r", [DEPTH, 12, 128, KC * 128])
    w_v_d = din("w_v_r", [DEPTH, 2, 128, KC * 256])
    w_out_d = din("w_out_r", [DEPTH, 8, 128, KC * 128])
    w_up_d = din("w_up_r", [DEPTH, NJ, 128, KC * 256])
    w_dn_d = din("w_dn_r", [DEPTH, 8, 128, NJ * 128])

    yT_d = dout("yT", [128, KC, main_tok])
    ysT_d = dout("ysT", [128, KC, SAMP])
    o_pool = {"m": dout("pool_p", [128, DEPTH, 2, 16]), "s": dout("pool_s", [128, DEPTH, 2, 16])}
    o_conv = {"m": dout("conv_p", [128, DEPTH, 2, 2]), "s": dout("conv_s", [128, DEPTH, 2, 2])}
    o_ffn = {"m": dout("ffn_p", [128, DEPTH, NCH, 2]), "s": dout("ffn_s", [128, DEPTH, NCH, 2])}
    o_v = dout("v_s", [SAMP, DEPTH, 512])

    def sb(name, shape, dt):
        return nc.alloc_sbuf_tensor(name, list(shape), dt)

    cst = sb("cst_t", [128, NCST], F32)
    wsT = sb("wsT_b", [128, DEPTH * 4, 128], BF16)
    bs2 = sb("bs2_b", [2, DEPTH * 4 * 128], BF16)
    bs2v = bs2[:, :].rearrange("p (i q) -> p i q", i=DEPTH * 4)
    wpool = sb("wpool_b", [128, DEPTH * 2, 128], BF16)
    ident = sb("ident_t", [128, 128], F32)
    ones_b = sb("ones_b", [128, 128], BF16)
    ring = sb("ring", [128, NSLOT, SLOT], BF16)

    def mkseg(name, T, maxn):
        g = Seg()
        g.name = name
        g.T = T
        g.maxn = maxn
        g.xb = [sb(name + "_x0", [128, KC, T], F32)]
        g.xb.append(sb(name + "_x1", [128, KC, T], F32) if name == "m" else g.xb[0])
        g.xcur = g.xb[0]
        g.xn = "x0"
        g.off = 0
        g.h = sb(name + "_h", [128, KC, 2 + T], BF16)
        g.ycat = sb(name + "_yc", [128, KC, T], BF16)
        g.rstd = sb(name + "_rs", [128, T], F32)
        g.phist = sb(name + "_ph", [128, DEPTH, 2, 16], F32)
        g.qhist = sb(name + "_qh", [128, DEPTH, 2, 2], F32)
        g.hhist = sb(name + "_hh", [128, DEPTH, KC, 2], BF16)
        g.upst = sb(name + "_us", [128, DEPTH, NCH, 2], F32)
        return g

    NF = 3
    FW = 448
    main = mkseg("m", TMAX, 448)
    T = TMAX
    mixer_bytes = 4 * 4 * T + 2 * (T // 128) * 512 + 3 * 4 * 2 * (16 + T) + 2 * 2 * T + 4 * 2 * T + 4 * 2 * (2 + T)
    ffn_bytes = 2 * NJ * T + 3 * NF * 4 * FW
    y_bytes = 4 * KC * T
    ub = max(mixer_bytes, ffn_bytes, y_bytes)
    ub = (ub + 63) // 64 * 64
    union = sb("union", [128, ub // 4], F32)

    def carve(seg_bufs, base_tensor, dt_bytes_total):
        pass

    class Carver:
        def __init__(self, t):
            self.t = t
            self.off = 0

        def reset(self):
            self.off = 0

        def take(self, shape, dt):
            esz = 4 if dt == F32 else 2
            n = int(np.prod(shape[1:]))
            self.off = (self.off + 31) // 32 * 32
            o = self.off
            self.off += n * esz
            assert self.off <= ub, (self.off, ub)
            if dt == F32:
                ap = self.t[:, o // 4:o // 4 + n]
            else:
                ap = self.t[:].bitcast(BF16)[:, o // 2:o // 2 + n]
            if len(shape) == 3:
                ap = ap.rearrange("p (a b) -> p a b", a=shape[1])
            return ap

    cv = Carver(union)
    nchunk_m = T // 128
    main.u = cv.take([128, 4, T], F32)
    main.vt = cv.take([128, nchunk_m, 512], BF16)
    main.P = cv.take([128, 2, 16 + T], F32)
    main.A1 = cv.take([128, 2, 16 + T], F32)
    main.A2 = cv.take([128, 2, 16 + T], F32)
    main.d = cv.take([128, 2, T], BF16)
    main.acc = cv.take([128, 2, T], F32)
    main.Q = cv.take([128, 2, 2 + T], F32)
    cv.reset()
    main.hid = cv.take([128, NJ, T], BF16)
    fscr = [[cv.take([128, FW], F32) for _ in range(NF)] for _ in range(3)]
    cv.reset()
    wsT_f = cv.take([128, DEPTH * 4, 128], F32)
    mask_t = cv.take([128, 128], F32)
    bs_f = cv.take([128, DEPTH * 4 * 128], F32)[0:2, :]
    bs_h = cv.take([128, DEPTH * 4 * 128], BF16)[0:2, :]
    bs_r = cv.take([128, DEPTH * 4 * 128], F32)[0:2, :]
    main.region = True
    main.vf = None

    samp = None
    if with_sample:
        samp = mkseg("s", SAMP, SAMP)
        samp.u = sb("s_u", [128, 4, SAMP], F32)
        samp.vt = sb("s_vt", [128, 1, 512], BF16)
        samp.vf = sb("s_vf", [SAMP, 512], F32)
        samp.P = sb("s_P", [128, 2, 16 + SAMP], F32)
        samp.A1 = sb("s_A1", [128, 2, 16 + SAMP], F32)
        samp.A2 = sb("s_A2", [128, 2, 16 + SAMP], F32)
        samp.d = sb("s_d", [128, 2, SAMP], BF16)
        samp.acc = sb("s_acc", [128, 2, SAMP], F32)
        samp.Q = sb("s_Q", [128, 2, 2 + SAMP], F32)
        samp.hid = sb("s_hid", [128, NJ, SAMP], BF16)
        samp.region = False

    ps = nc.alloc_psum_tensor("ps", [128, 8, 512], F32) if hasattr(nc, "alloc_psum_tensor") else None
    assert ps is not None

    sem_names = ["pe", "act", "dve", "pool", "init", "initg", "xld0_0", "xld0_1", "xld1_0", "xld1_1", "misc", "vst"] + ["ring%d" % i for i in range(NSLOT)] + \
                ["st%d" % i for i in range(4)]
    for n in sem_names:
        S.add_sem(n)

    def C(off, n=1):
        return cst[:, off:off + n]

    bank_ctr = [0]

    def next_bank():
        b = bank_ctr[0] % 8
        bank_ctr[0] += 1
        return b

    fs_ctr = [0]

    wchunk_ctr = [0]
    wissued = [0]
    wseq = []
    for _ti in range(NT):
        _nu = len(subtiles(tiles[_ti], 448)) + (1 if (with_sample and _ti == NT - 1) else 0)
        _np = 2 if (STAGGER and _nu > 1) else 1
        for _l in range(DEPTH):
            for _w in (8, 9, 10, 11, 4, 5):
                wseq.append((w_in_d[_l, _w], KC * 128))
            for _hf in range(2):
                wseq.append((w_v_d[_l, _hf], KC * 256))
            for _w in (0, 1, 2, 3, 6, 7):
                wseq.append((w_in_d[_l, _w], KC * 128))
            for _p in range(_np):
                for _m in range(8):
                    wseq.append((w_out_d[_l, _m], KC * 128))
            for _j in range(NJ):
                wseq.append((w_up_d[_l, _j], KC * 256))
            for _p in range(_np):
                for _m in range(8):
                    for _hh in range(2):
                        wseq.append((w_dn_d[_l, _m][:, _hh * HK * 128:(_hh + 1) * HK * 128], HK * 128))

    def wissue_upto(n):
        while wissued[0] < min(n, len(wseq)):
            i = wissued[0]
            wissued[0] += 1
            slot = i % NSLOT
            src, nelem = wseq[i]
            dst = ring[:, slot, 0:nelem]
            S.op("pool", lambda e, dst=dst, src=src: e.dma_start(out=dst, in_=src),
                 writes=["ring%d" % slot], sem="ring%d" % slot, inc=16)

    wopen = []

    def wrelease():
        del wopen[:]
        wissue_upto(wchunk_ctr[0] + PF)

    def wload(dram_ap, nelem, keep=False):
        i = wchunk_ctr[0]
        wchunk_ctr[0] += 1
        assert wseq[i][1] == nelem, (i, wseq[i][1], nelem)
        if not keep:
            del wopen[:]
        wopen.append(i)
        assert i - wopen[0] < NSLOT
        wissue_upto(min(i + PF, wopen[0] + NSLOT))
        slot = i % NSLOT
        return slot, ring[:, slot, 0:nelem]

    def mm_group(mms, reads, writes, region=False):
        def fn(e, mms=mms):
            ins = None
            for (o, l, r, st, sp) in mms:
                ins = e.matmul(o, l, r, start=st, stop=sp)
            return ins
        return S.op("pe", fn, reads=reads, writes=writes, region=region)

    def init():
        loads = [(cst[:], cst_d), (wsT_f, wsT_d), (mask_t, mask_d), (bs_f, bs2_d), (ident[:], ident_d)]
        for (dst, src) in loads:
            S.op("sp", lambda e, dst=dst, src=src: e.dma_start(out=dst, in_=src), writes=[], sem="init", inc=16)
        S.op("pool", lambda e: e.dma_start(out=wpool[:], in_=wpool_d), writes=[], sem="initg", inc=16)
        for g in ([main] + ([samp] if samp else [])):
            if g is main:
                S.op("pool", lambda e, g=g: e.memset(g.phist[:], 0.0), writes=[])
                S.op("pool", lambda e, g=g: e.memset(g.qhist[:], 0.0), writes=[])
            else:
                S.op("sp", lambda e, g=g: e.dma_start(out=g.phist[:], in_=spool_d), writes=[], sem="init", inc=16)
                S.op("sp", lambda e, g=g: e.dma_start(out=g.qhist[:], in_=sconv_d), writes=[], sem="init", inc=16)
                S.op("sp", lambda e, g=g: e.dma_start(out=g.upst[:], in_=sffn_d), writes=[], sem="init", inc=16)
                S.op("sp", lambda e, g=g: e.dma_start(out=g.xb[0][:], in_=xsT), writes=[], sem="init", inc=16)
            S.op("pool", lambda e, g=g: e.memset(g.hhist[:], 0.0), writes=[])
        S.op("pool", lambda e: e.memset(ones_b[:], 1.0), writes=[])
        for eng in ("pe", "act", "dve", "pool"):
            S.wait_all(eng, ["init", "initg", "pool"])
        for i in range(DEPTH * 4):
            S.op("dve", lambda e, i=i: e.tensor_tensor(out=wsT[:, i, :], in0=wsT_f[:, i, :], in1=mask_t,
                                                     op=ALU.mult), writes=["wsT"], region=True)
        S.op("dve", lambda e: e.tensor_copy(out=bs_h, in_=bs_f), writes=["bs_h"], region=True)
        S.op("dve", lambda e: e.tensor_copy(out=bs_r, in_=bs_h), reads=["bs_h"], writes=["bs_r"], region=True)
        S.op("dve", lambda e: e.tensor_tensor(out=bs_r, in0=bs_f, in1=bs_r, op=ALU.subtract),
             reads=["bs_r"], writes=["bs_r"], region=True)
        S.op("dve", lambda e: e.tensor_scalar(out=bs_f, in0=bs_h, scalar1=cst[0:2, O_M01:O_M01 + 1],
                                              scalar2=None, op0=ALU.mult), reads=["bs_h", "bs_r"], writes=["bs_f"],
             region=True)
        S.op("dve", lambda e: e.scalar_tensor_tensor(out=bs2[:], in0=bs_r, scalar=cst[0:2, O_M01 + 1:O_M01 + 2],
                                                     in1=bs_f, op0=ALU.mult, op1=ALU.add),
             reads=["bs_f", "bs_r"], writes=["bs2"], region=True)

    def rn(g, *a):
        return g.name + "." + ".".join(str(x) for x in a)

    def cur_x(g):
        return (g.xcur, g.xn)

    def norm_sq(g, si, c0, c1, xb):
        x, xn = xb
        S.op("act", lambda e, g=g, c0=c0, c1=c1, x=x: e.activation(out=g.ycat[:, :, c0:c1], in_=x[:, :, c0:c1],
                                                                   func=AF.Square),
             reads=[rn(g, xn, k, si) for k in range(KC)], writes=[rn(g, "yc", k, si) for k in range(KC)])

    def norm_rest(g, si, c0, c1, xb, goff, inplace_y=False, sq="yc"):
        x, xn = xb
        n = c1 - c0
        b = next_bank()
        sqt = g.ycat if sq == "yc" else g.hid
        mm_group([(ps[:, b, 0:n], ones_b[:], sqt[:, k, c0:c1], k == 0, k == KC - 1) for k in range(KC)],
                 reads=[rn(g, sq, k, si) for k in range(KC)], writes=["ps%d" % b],
                 region=(sq == "hid" and g.region))
        S.op("act", lambda e, g=g, b=b, n=n, c0=c0, c1=c1: e.activation(out=g.rstd[:, c0:c1], in_=ps[:, b, 0:n],
                                                                        func=AF.Sqrt, bias=C(O_EPS), scale=1.0 / D),
             reads=["ps%d" % b], writes=[rn(g, "rs", si)])
        S.op("dve", lambda e, g=g, c0=c0, c1=c1: e.reciprocal(out=g.rstd[:, c0:c1], in_=g.rstd[:, c0:c1]),
             reads=[rn(g, "rs", si)], writes=[rn(g, "rs", si)])
        for k in range(KC):
            if inplace_y:
                o = x[:, k, c0:c1]
                wr = [rn(g, xn, k, si)]
            else:
                o = g.h[:, k, 2 + c0:2 + c1]
                wr = [rn(g, "h", k, si)]
            S.op("dve", lambda e, g=g, o=o, k=k, c0=c0, c1=c1, x=x: e.scalar_tensor_tensor(
                out=o, in0=x[:, k, c0:c1], scalar=C(goff + k), in1=g.rstd[:, c0:c1],
                op0=ALU.mult, op1=ALU.mult),
                reads=[rn(g, xn, k, si), rn(g, "rs", si)], writes=wr)

    def norm_units(us, xsel, goff, inplace_y=False):
        for (g, si, c0, c1) in us:
            norm_sq(g, si, c0, c1, xsel(g))
            norm_rest(g, si, c0, c1, xsel(g), goff, inplace_y)

    CU = [None]
    PENDING = []

    def flush_pending():
        while PENDING:
            PENDING.pop(0)()


    def units_of(ti, l, segs):
        us = []
        for g in segs:
            off = HOFF[l] if (TRIM and g is main and ti == 0) else 0
            for si, (c0, c1) in enumerate(g.subs):
                if c1 > off:
                    us.append((g, si, max(c0, off), c1))
        return us

    def out_proj(segs, l, wd, nk, srcname, src_of, next_norm, extra_after_a=None, region=False):
        us = CU[0]
        groups = [us[:1], us[1:]] if (STAGGER and len(us) > 1) else [us]
        if next_norm is not None:
            nus, nxsel, ngoff = next_norm
            ngroups = [nus[:1], nus[1:]] if len(groups) > 1 else [nus]
        presq = PRESQ and len(groups) > 1
        sqname = "yc" if nk == NJ else "hid"
        presq_cnt = {}
        a_presq = False
        nmap = {}
        if presq and next_norm is not None:
            nmap = {(g.name, si): (c0, c1) for (g, si, c0, c1) in ngroups[0] + ngroups[1]}
        for gi, ug in enumerate(groups):
            for m in range(KC):
                if nk == KC:
                    slot, w = wload(wd[l, m], nk * 128)
                    wv = w.rearrange("p (k m) -> p k m", k=nk)
                    wk = [wv[:, k, :] for k in range(nk)]
                    slots = [slot]
                else:
                    wk = []
                    slots = []
                    for hh in range(2):
                        slot, w = wload(wd[l, m][:, hh * HK * 128:(hh + 1) * HK * 128], HK * 128, keep=(hh > 0))
                        wv = w.rearrange("p (k m) -> p k m", k=HK)
                        wk += [wv[:, k, :] for k in range(HK)]
                        slots.append(slot)
                for (g, si, c0, c1) in ug:
                    n = c1 - c0
                    b = next_bank()
                    src = src_of(g)
                    mm_group([(ps[:, b, 0:n], wk[k], src[:, k, c0:c1], k == 0, k == nk - 1) for k in range(nk)],
                             reads=["ring%d" % sl for sl in slots] + [rn(g, srcname, k, si) for k in range(nk)],
                             writes=["ps%d" % b], region=(region and g.region))
                    S.op("dve", lambda e, g=g, m=m, b=b, n=n, c0=c0, c1=c1, x=g.xcur: e.tensor_tensor(
                        out=x[:, m, c0:c1], in0=ps[:, b, 0:n], in1=x[:, m, c0:c1], op=ALU.add),
                        reads=["ps%d" % b, rn(g, g.xn, m, si)], writes=[rn(g, g.xn, m, si)])
                    if presq and next_norm is not None and (g.name, si) in nmap:
                        x_, xn_ = nxsel(g)
                        if xn_ == g.xn:
                            c0n, c1n = nmap[(g.name, si)]
                            sqt = g.ycat if sqname == "yc" else g.hid
                            S.op("act", lambda e, sqt=sqt, m=m, c0n=c0n, c1n=c1n, x_=x_: e.activation(
                                out=sqt[:, m, c0n:c1n], in_=x_[:, m, c0n:c1n], func=AF.Square),
                                reads=[rn(g, xn_, m, si)], writes=[rn(g, sqname, m, si)],
                                region=(sqname == "hid" and g.region))
                            presq_cnt[(g.name, si)] = presq_cnt.get((g.name, si), 0) + 1
                if len(groups) > 1 and gi == 1 and m == (0 if a_presq else 1) and next_norm is not None:
                    for (g, si, c0, c1) in ngroups[0]:
                        norm_rest(g, si, c0, c1, nxsel(g), ngoff,
                                  sq=(sqname if presq_cnt.get((g.name, si), 0) == KC else "yc"))
            if gi == 0:
                if extra_after_a is not None:
                    extra_after_a()
                if next_norm is not None:
                    if len(groups) > 1:
                        a_presq = all(presq_cnt.get((g.name, si), 0) == KC for (g, si, c0, c1) in ngroups[0])
                        for (g, si, c0, c1) in ngroups[0]:
                            if presq_cnt.get((g.name, si), 0) != KC:
                                norm_sq(g, si, c0, c1, nxsel(g))
                    else:
                        norm_units(nus, nxsel, ngoff)
            elif next_norm is not None:
                sqs = {}
                for (g, si, c0, c1) in ngroups[1]:
                    if presq_cnt.get((g.name, si), 0) != KC:
                        norm_sq(g, si, c0, c1, nxsel(g))
                        sqs[(g.name, si)] = "yc"
                    else:
                        sqs[(g.name, si)] = sqname
                PENDING.append(lambda us=ngroups[1], nxsel=nxsel, ngoff=ngoff, sqs=sqs: [
                    norm_rest(g, si, c0, c1, nxsel(g), ngoff, sq=sqs[(g.name, si)]) for (g, si, c0, c1) in us])

    def mixer(segs, l, first_tile):
        for g in segs:
            S.op("act", lambda e, g=g: e.activation(out=g.P[:, :, 0:16], in_=g.phist[:, l, :, :], func=AF.Copy),
                 reads=[rn(g, "ph", l)], writes=[rn(g, "P", "hist")], region=g.region)
            S.op("act", lambda e, g=g: e.activation(out=g.Q[:, :, 0:2], in_=g.qhist[:, l, :, :], func=AF.Copy),
                 reads=[rn(g, "qh", l)], writes=[rn(g, "Q", "hist")], region=g.region)
        S.op("act", lambda e: e.activation(out=main.h[:, :, 0:2], in_=main.hhist[:, l, :, :], func=AF.Copy),
             reads=[rn(main, "hh", l)], writes=[rn(main, "h", "hist")])

        def hreads(g, si):
            return [rn(g, "h", k, si) for k in range(KC)]

        def fm_chunk(widx, consume):
            slot, w = wload(w_in_d[l, widx], KC * 128)
            wv = w.rearrange("p (k m) -> p k m", k=KC)
            for (g, si, c0, c1) in CU[0]:
                if True:
                    n = c1 - c0
                    b = next_bank()
                    mm_group([(ps[:, b, 0:n], wv[:, k, :], g.h[:, k, 2 + c0:2 + c1], k == 0, k == KC - 1)
                              for k in range(KC)],
                             reads=["ring%d" % slot] + hreads(g, si), writes=["ps%d" % b])
                    consume(g, si, c0, c1, b, n)

        us_all = CU[0]
        ugroups = [us_all[:1], us_all[1:]] if (STAGGER and len(us_all) > 1) else [us_all]
        fw = []
        for i_, widx in enumerate((8, 9, 10, 11, 4, 5)):
            slot, w = wload(w_in_d[l, widx], KC * 128, keep=(i_ > 0))
            fw.append((slot, w.rearrange("p (k m) -> p k m", k=KC)))
        flush_pending()
        for gi_, ug in enumerate(ugroups):
            for i_ in range(6):
                slot, wv = fw[i_]
                j = i_ % 2
                for (g, si, c0, c1) in ug:
                    n = c1 - c0
                    b = next_bank()
                    mm_group([(ps[:, b, 0:n], wv[:, k, :], g.h[:, k, 2 + c0:2 + c1], k == 0, k == KC - 1)
                              for k in range(KC)],
                             reads=["ring%d" % slot] + hreads(g, si), writes=["ps%d" % b])
                    if i_ < 2:
                        S.op("act", lambda e, g=g, j=j, c0=c0, c1=c1, b=b, n=n: e.activation(
                            out=g.acc[:, j, c0:c1], in_=ps[:, b, 0:n], func=AF.Copy),
                            reads=["ps%d" % b], writes=[rn(g, "acc", j, si)], region=g.region)
                    elif i_ >= 4:
                        S.op("act", lambda e, g=g, j=j, c0=c0, c1=c1, b=b, n=n: e.activation(
                            out=g.P[:, j, 16 + c0:16 + c1], in_=ps[:, b, 0:n], func=AF.Copy),
                            reads=["ps%d" % b], writes=[rn(g, "P", j, si)], region=g.region)
                    else:
                        S.op("dve", lambda e, g=g, j=j, c0=c0, c1=c1, b=b, n=n: e.tensor_tensor(
                            out=g.Q[:, j, 2 + c0:2 + c1], in0=ps[:, b, 0:n], in1=g.acc[:, j, c0:c1], op=ALU.mult),
                            reads=["ps%d" % b, rn(g, "acc", j, si)], writes=[rn(g, "Q", j, si)], region=g.region)
                if gi_ == 0 and i_ == 1:
                    flush_pending()
        flush_pending()
        wrelease()
        for g in segs:
            ns = len(g.subs)
            for j in range(2):
                allQ = [rn(g, "Q", "hist")] + [rn(g, "Q", j, si) for si in range(ns)]
                wc = O_WCV + (l * 2 + j) * 3
                accw = [rn(g, "acc", j, si) for si in range(ns)]
                S.op("act", lambda e, g=g, j=j, wc=wc, T_=g.T: e.activation(out=g.acc[:, j, 0:T_], in_=g.Q[:, j, 2:2 + T_],
                                                                    func=AF.Identity, scale=C(wc + 2)),
                     reads=allQ, writes=accw, region=g.region)
                S.op("dve", lambda e, g=g, j=j, wc=wc, T_=g.T: e.scalar_tensor_tensor(
                    out=g.acc[:, j, 0:T_], in0=g.Q[:, j, 1:1 + T_], scalar=C(wc + 1), in1=g.acc[:, j, 0:T_],
                    op0=ALU.mult, op1=ALU.add), reads=allQ + accw, writes=accw, region=g.region)
                S.op("dve", lambda e, g=g, j=j, wc=wc, T_=g.T: e.scalar_tensor_tensor(
                    out=g.acc[:, j, 0:T_], in0=g.Q[:, j, 0:T_], scalar=C(wc), in1=g.acc[:, j, 0:T_],
                    op0=ALU.mult, op1=ALU.add), reads=allQ + accw, writes=accw, region=g.region)
            allQ2 = [rn(g, "Q", "hist")] + [rn(g, "Q", j, si) for j in range(2) for si in range(ns)]
            S.op("act", lambda e, g=g, T_=g.T: e.activation(out=g.qhist[:, l, :, :], in_=g.Q[:, :, T_:T_ + 2], func=AF.Copy),
                 reads=allQ2, writes=[rn(g, "qh", l)], region=g.region)
        for g in segs:
            W = 16 + g.T
            allP = [rn(g, "P", "hist")] + [rn(g, "P", j, si) for j in range(2) for si in range(len(g.subs))]
            S.op(CHAIN_ENG, lambda e, g=g, W=W: e.tensor_tensor(out=g.A1[:, :, 1:W], in0=g.P[:, :, 1:W],
                                                            in1=g.P[:, :, 0:W - 1], op=ALU.add),
                 reads=allP, writes=[rn(g, "A1", 0), rn(g, "A1", 1)], region=g.region)
            S.op(CHAIN_ENG, lambda e, g=g, W=W: e.tensor_tensor(out=g.A2[:, :, 3:W], in0=g.A1[:, :, 3:W],
                                                            in1=g.A1[:, :, 1:W - 2], op=ALU.add),
                 reads=[rn(g, "A1", 0), rn(g, "A1", 1)], writes=[rn(g, "A2", 0), rn(g, "A2", 1)], region=g.region)
            S.op(CHAIN_ENG, lambda e, g=g, W=W: e.tensor_tensor(out=g.A1[:, 1, 7:W], in0=g.A2[:, 1, 7:W],
                                                            in1=g.A2[:, 1, 3:W - 4], op=ALU.add),
                 reads=[rn(g, "A2", 1)], writes=[rn(g, "A1", 1)], region=g.region)
            S.op(CHAIN_ENG, lambda e, g=g, W=W: e.tensor_tensor(out=g.A2[:, 1, 15:W], in0=g.A1[:, 1, 15:W],
                                                            in1=g.A1[:, 1, 7:W - 8], op=ALU.add),
                 reads=[rn(g, "A1", 1)], writes=[rn(g, "A2", 1)], region=g.region)
            S.op("act", lambda e, g=g, T_=g.T: e.activation(out=g.phist[:, l, :, :], in_=g.P[:, :, T_:T_ + 16],
                                                    func=AF.Copy),
                 reads=allP, writes=[rn(g, "ph", l)], region=g.region)
        def pool_d(g):
            allP = [rn(g, "P", "hist")] + [rn(g, "P", j, si) for j in range(2) for si in range(len(g.subs))]
            for j in range(2):
                for half in range(2):
                    src = g.A1 if half == 0 else g.A2
                    p0, p1 = half * 64, half * 64 + 64
                    S.op("dve", lambda e, g=g, j=j, src=src, p0=p0, p1=p1, T_=g.T: e.scalar_tensor_tensor(
                        out=g.d[p0:p1, j, 0:T_], in0=src[p0:p1, j, 16:16 + T_], scalar=cst[p0:p1, O_INVW + j:O_INVW + j + 1],
                        in1=g.P[p0:p1, j, 16:16 + T_], op0=ALU.mult, op1=ALU.subtract),
                        reads=allP + [rn(g, "A1", j), rn(g, "A2", j)], writes=[rn(g, "d", j, half)], region=g.region)
                    if first_tile and g is main:
                        a0 = 16 + HALO
                        S.op("dve", lambda e, g=g, j=j, src=src, p0=p0, p1=p1, a0=a0: e.tensor_tensor(
                            out=src[p0:p1, j, a0:a0 + 16], in0=src[p0:p1, j, a0:a0 + 16],
                            in1=cst[p0:p1, O_INVC + 16 * j:O_INVC + 16 * j + 16], op=ALU.mult),
                            reads=[rn(g, "d", j, half)], writes=[rn(g, "A1", j), rn(g, "A2", j)], region=g.region)
                        S.op("dve", lambda e, g=g, j=j, src=src, p0=p0, p1=p1, a0=a0: e.tensor_tensor(
                            out=g.d[p0:p1, j, HALO:HALO + 16], in0=src[p0:p1, j, a0:a0 + 16],
                            in1=g.P[p0:p1, j, a0:a0 + 16], op=ALU.subtract),
                            reads=allP + [rn(g, "A1", j), rn(g, "A2", j)], writes=[rn(g, "d", j, half)],
                            region=g.region)
        def pool_mm(g, j):
                for (g_, si, c0, c1) in CU[0]:
                    if g_ is not g:
                        continue
                    n = c1 - c0
                    b = next_bank()
                    mm_group([(ps[:, b, 0:n], wpool[:, l * 2 + j, :], g.d[:, j, c0:c1], True, True)],
                             reads=[rn(g, "d", j, 0), rn(g, "d", j, 1)], writes=["ps%d" % b], region=g.region)
                    S.op("act", lambda e, g=g, j=j, b=b, n=n, c0=c0, c1=c1: e.activation(
                        out=g.ycat[:, 4 + j, c0:c1], in_=ps[:, b, 0:n], func=AF.Identity,
                        scale=C(O_PSC + l * 2 + j)),
                        reads=["ps%d" % b], writes=[rn(g, "yc", 4 + j, si)])
        vslots = []
        for hf in range(2):
            slot, w = wload(w_v_d[l, hf], KC * 256, keep=(hf > 0))
            vslots.append((slot, w.rearrange("p (k m) -> p k m", k=KC)))
        for g in segs:
            g.chunks = [(t0, min(128, g.T - t0)) for t0 in range(0, g.T, 128) if t0 >= g.off]
            for (t0, L) in g.chunks:
                ci = t0 // 128
                b = next_bank()
                sis = sorted(set(si for si, (c0, c1) in enumerate(g.subs) if c0 < t0 + L and c1 > t0))
                rd = []
                for si in sis:
                    rd += hreads(g, si)
                mms = []
                for hf in range(2):
                    for k in range(KC):
                        mms.append((ps[0:L, b, hf * 256:(hf + 1) * 256], g.h[:, k, 2 + t0:2 + t0 + L],
                                    vslots[hf][1][:, k, :], k == 0, k == KC - 1))
                mm_group(mms, reads=["ring%d" % vslots[0][0], "ring%d" % vslots[1][0]] + rd, writes=["ps%d" % b])
                S.op("act", lambda e, g=g, ci=ci, L=L, b=b: e.activation(out=g.vt[0:L, ci, :], in_=ps[0:L, b, :],
                                                                         func=AF.Copy),
                     reads=["ps%d" % b], writes=[rn(g, "vt", ci)], region=g.region)
                if g.vf is not None:
                    S.op("act", lambda e, g=g, L=L, b=b: e.activation(out=g.vf[0:L, :], in_=ps[0:L, b, :],
                                                                      func=AF.Copy),
                         reads=["ps%d" % b], writes=[rn(g, "vf")])
                    S.op("sp", lambda e, g=g: e.dma_start(out=o_v[:, l, :], in_=g.vf[:]),
                         reads=[rn(g, "vf")], writes=["vst"], sem="vst", inc=16)
        for m in range(4):
            def cons_u(g, si, c0, c1, b, n, m=m):
                S.op("act", lambda e: e.activation(out=g.u[:, m, c0:c1], in_=ps[:, b, 0:n], func=AF.Copy),
                     reads=["ps%d" % b], writes=[rn(g, "u", m, si)], region=g.region)
            fm_chunk(m, cons_u)
        for g in segs:
            pool_d(g)
        for g in segs:
            for hd in range(4):
                for grp in range(0, len(g.chunks), 4):
                    cs = g.chunks[grp:grp + 4]
                    b = next_bank()
                    mms = []
                    rd = ["wsT", "bs2"]
                    tot = 0
                    Lc = cs[0][1]
                    nch = len(cs)
                    bo = (l * 4 + hd) * 128
                    brhs = bs2[0:2, bo:bo + Lc]
                    if nch > 1:
                        brhs = bs2v[0:2, l * 4 + hd:l * 4 + hd + 1, 0:Lc].broadcast_to([2, nch, Lc])
                        bout = ps[:, b, 0:nch * Lc].rearrange("p (c q) -> p c q", c=nch)
                    else:
                        bout = ps[:, b, 0:Lc]
                    mms.append((bout, ones_b[0:2, :], brhs, True, False))
                    for i_c, (t0, L) in enumerate(cs):
                        ci = t0 // 128
                        o = ps[:, b, tot:tot + L]
                        mms.append((o, g.vt[0:L, ci, hd * 128:(hd + 1) * 128], wsT[0:L, l * 4 + hd, 0:L], False,
                                    i_c == nch - 1))
                        rd.append(rn(g, "vt", ci))
                        tot += L
                    mm_group(mms, reads=rd, writes=["ps%d" % b], region=g.region)
                    q0 = cs[0][0]
                    sis = sorted(set(si for si, (c0, c1) in enumerate(g.subs) if c0 < q0 + tot and c1 > q0))
                    S.op("dve", lambda e, g=g, hd=hd, b=b, q0=q0, tot=tot: e.tensor_tensor(
                        out=g.ycat[:, hd, q0:q0 + tot], in0=ps[:, b, 0:tot], in1=g.u[:, hd, q0:q0 + tot], op=ALU.mult),
                        reads=["ps%d" % b] + [rn(g, "u", hd, si) for si in sis],
                        writes=[rn(g, "yc", hd, si) for si in sis], region=g.region)
        for g in segs:
            for j in range(2):
                pool_mm(g, j)
        for j in range(2):
            def cons_gb(g, si, c0, c1, b, n, j=j):
                S.op("dve", lambda e: e.tensor_tensor(out=g.ycat[:, 6 + j, c0:c1], in0=ps[:, b, 0:n],
                                                      in1=g.acc[:, j, c0:c1], op=ALU.mult),
                     reads=["ps%d" % b] + [rn(g, "acc", j, s2) for s2 in range(len(g.subs))],
                     writes=[rn(g, "yc", 6 + j, si)], region=g.region)
            fm_chunk(6 + j, cons_gb)
        out_proj(segs, l, w_out_d, KC, "yc", lambda g: g.ycat, (CU[0], cur_x, O_G2 + l * KC))

    def ffn(segs, l, first_tile, save_state, next_norm):
        for g in segs:
            ns = len(g.subs)
            pass

        def hh_save():
            g = main
            ns = len(g.subs)
            S.op("act", lambda e, g=g, T_=g.T: e.activation(out=g.hhist[:, l, :, :], in_=g.h[:, :, T_:T_ + 2],
                                                    func=AF.Copy),
                 reads=[rn(g, "h", k, ns - 1) for k in range(KC)], writes=[rn(g, "hh", l)])
        us_all = CU[0]
        NFRONT = 4 if (STAGGER and len(us_all) > 1) else 0
        plan = []
        for ug in ([us_all[:1], us_all[1:]] if NFRONT else []):
            for j in range(NFRONT):
                for u_ in ug:
                    plan.append((j, u_))
        NBACK = 2 if NFRONT else 0
        for j in range(NFRONT, NJ - NBACK):
            for u_ in us_all:
                plan.append((j, u_))
        for ug in ([us_all[:1], us_all[1:]] if NBACK else []):
            for j in range(NJ - NBACK, NJ):
                for u_ in ug:
                    plan.append((j, u_))
        upw = {}
        flush_pending()
        hh_save()
        for pi_, (j, (g, si, c0, c1)) in enumerate(plan):
            if j not in upw:
                if j < NFRONT:
                    blk = list(range(NFRONT))
                elif NBACK and j >= NJ - NBACK:
                    blk = list(range(NJ - NBACK, NJ))
                else:
                    blk = [j]
                for jj in blk:
                    slot, w = wload(w_up_d[l, jj], KC * 256, keep=(jj != blk[0]))
                    upw[jj] = (slot, w.rearrange("p (k t m) -> p k t m", k=KC, t=2))
            slot, wv = upw[j]
            if True:
                ns = len(g.subs)
                if True:
                    n = c1 - c0
                    hr = [rn(g, "h", k, si) for k in range(KC)]
                    if g is main:
                        hr += [rn(g, "h", k, si - 1) for k in range(KC)] if si > 0 else [rn(g, "h", "hist")]
                    banks = []
                    for t in range(2):
                        b = next_bank()
                        banks.append(b)
                        c = t * NJ + j
                        mms = []
                        if g is main:
                            mms += [(ps[:, b, 0:n + 2], wv[:, k, t, :], g.h[:, k, c0:c1 + 2], k == 0, k == KC - 1)
                                    for k in range(KC)]
                        else:
                            mms.append((ps[:, b, 0:2], ident[:], g.upst[:, l, c, :], True, True))
                            mms += [(ps[:, b, 2:n + 2], wv[:, k, t, :], g.h[:, k, 2 + c0:2 + c1], k == 0, k == KC - 1)
                                    for k in range(KC)]
                        mm_group(mms, reads=["ring%d" % slot] + hr + ([rn(g, "ups", l, c)] if g is not main else []),
                                 writes=["ps%d" % b])
                    fi = fs_ctr[0] % NF
                    fs_ctr[0] += 1
                    bufs = [fscr[t][fi][:, 0:n] for t in range(2)]
                    sbuf_ = fscr[2][fi][:, 0:n]
                    for t in range(2):
                        b = banks[t]
                        c = t * NJ + j
                        wo = O_WF + (l * NCH + c) * 3
                        bo = O_BF + l * NCH + c
                        buf = bufs[t]
                        rname = "f%d.%d" % (t, fi)
                        if save_state and si == ns - 1:
                            S.op("act", lambda e, g=g, b=b, n=n, c=c: e.activation(out=g.upst[:, l, c, :],
                                                                                  in_=ps[:, b, n:n + 2], func=AF.Copy),
                                 reads=["ps%d" % b], writes=[rn(g, "ups", l, c)])
                        S.op("act", lambda e, b=b, n=n, buf=buf, wo=wo, bo=bo: e.activation(
                            out=buf, in_=ps[:, b, 2:n + 2], func=AF.Identity, bias=C(bo), scale=C(wo + 2)),
                            reads=["ps%d" % b], writes=[rname], region=True)
                    for t in range(2):
                        b = banks[t]
                        c = t * NJ + j
                        wo = O_WF + (l * NCH + c) * 3
                        buf = bufs[t]
                        rname = "f%d.%d" % (t, fi)
                        S.op("dve", lambda e, b=b, n=n, buf=buf, wo=wo: e.scalar_tensor_tensor(
                            out=buf, in0=ps[:, b, 1:n + 1], scalar=C(wo + 1), in1=buf, op0=ALU.mult, op1=ALU.add),
                            reads=["ps%d" % b, rname], writes=[rname], region=True)
                        S.op("dve", lambda e, b=b, n=n, buf=buf, wo=wo: e.scalar_tensor_tensor(
                            out=buf, in0=ps[:, b, 0:n], scalar=C(wo), in1=buf, op0=ALU.mult, op1=ALU.add),
                            reads=["ps%d" % b, rname], writes=[rname], region=True)
                        if t == 0:
                            S.op("act", lambda e, buf=buf, sbuf_=sbuf_: e.activation(out=sbuf_, in_=buf, func=AF.Silu),
                                 reads=[rname], writes=["f2.%d" % fi], region=True)
                    S.op(MUL_ENG, lambda e, g=g, j=j, c0=c0, c1=c1, sbuf_=sbuf_, abuf=bufs[1]: e.tensor_tensor(
                        out=g.hid[:, j, c0:c1], in0=sbuf_, in1=abuf, op=ALU.mult),
                        reads=["f2.%d" % fi, "f1.%d" % fi], writes=[rn(g, "hid", j, si)], region=g.region)
        def halo_flag():
            if first_tile and l < DEPTH - 1:
                g = main
                assert g.subs[0][1] >= HALO
                xs = [rn(g, g.xn, k, 0) for k in range(KC)]
                S.op("dve", lambda e, g=g, x=g.xcur: e.tensor_scalar(out=x[:, :, 0:HALO], in0=x[:, :, 0:HALO],
                                                                   scalar1=C(O_FLAG), scalar2=None, op0=ALU.mult),
                     reads=xs, writes=xs)

        out_proj(segs, l, w_dn_d, NJ, "hid", lambda g: g.hid, next_norm, extra_after_a=halo_flag, region=True)

    st_ctr = [0]

    def store(src_ap, dst_ap, reads, region=False):
        i = st_ctr[0] % 4
        st_ctr[0] += 1
        S.op("sp", lambda e, s=src_ap, d=dst_ap: e.dma_start(out=d, in_=s), reads=reads, writes=["stq%d" % i],
             sem="st%d" % i, inc=16, region=region)

    init()
    tok0s = [sum(tiles[:i]) for i in range(NT)]

    def load_x(ti):
        par = ti % 2
        T_ = tiles[ti]
        for si, (c0, c1) in enumerate(subtiles(T_, main.maxn)):
            xw = ["m.x%d.%d.%d" % (par, k, si) for k in range(KC)]
            S.op("sp", lambda e, c0=c0, c1=c1, t0=tok0s[ti], x=main.xb[par]: e.dma_start(
                out=x[:, :, c0:c1], in_=xT[:, :, t0 + c0:t0 + c1]),
                writes=xw, sem="xld%d_%d" % (par, si), inc=16)

    load_x(0)
    for ti, T in enumerate(tiles):
        tok0 = tok0s[ti]
        main.T = T
        main.xcur = main.xb[ti % 2]
        main.xn = "x%d" % (ti % 2)
        first = ti == 0
        last = ti == NT - 1
        segs = [main] + ([samp] if (samp is not None and last) else [])
        main.subs = subtiles(T, main.maxn)
        if samp is not None:
            samp.subs = subtiles(SAMP, SAMP)
        U = [units_of(ti, l, segs) for l in range(DEPTH)]
        if first:
            norm_units([u_ for u_ in U[0] if u_[0] is main], cur_x, O_G1)
        norm_units([u_ for u_ in U[0] if u_[0] is not main], cur_x, O_G1)
        for l in range(DEPTH):
            CU[0] = U[l]
            main.off = HOFF[l] if (TRIM and first) else 0
            S.phase_switch()
            mixer(segs, l, first)
            S.phase_switch()
            if l < DEPTH - 1:
                nn = (U[l + 1], cur_x, O_G1 + (l + 1) * KC)
            elif not last:
                load_x(ti + 1)
                npar = (ti + 1) % 2
                nsubs = subtiles(tiles[ti + 1], main.maxn)
                nus = [(main, si, c0, c1) for si, (c0, c1) in enumerate(nsubs)]
                nn = (nus, (lambda g, npar=npar: (g.xb[npar], "x%d" % npar)), O_G1)
            else:
                nn = None
            ffn(segs, l, first, save_state=last, next_norm=nn)
        S.phase_switch()
        flush_pending()
        for (g, si, c0, c1) in U[DEPTH - 1]:
            norm_sq(g, si, c0, c1, cur_x(g))
            norm_rest(g, si, c0, c1, cur_x(g), O_FG, inplace_y=True)
            yr = [rn(g, g.xn, k, si) for k in range(KC)]
            if g is main:
                a = max(c0, HALO if first else 0)
                if a < c1:
                    o0 = tok0 + a - HALO
                    store(g.xcur[:, :, a:c1], yT_d[:, :, o0:o0 + (c1 - a)], yr)
            else:
                store(g.xcur[:, :, :], ysT_d, yr)
    for g in ([main] + ([samp] if samp else [])):
        nm = g.name
        S.op("sp", lambda e, g=g, nm=nm: e.dma_start(out=o_pool[nm], in_=g.phist[:]),
             reads=[rn(g, "ph", l) for l in range(DEPTH)], sem="misc", inc=16)
        S.op("sp", lambda e, g=g, nm=nm: e.dma_start(out=o_conv[nm], in_=g.qhist[:]),
             reads=[rn(g, "qh", l) for l in range(DEPTH)], sem="misc", inc=16)
        S.op("sp", lambda e, g=g, nm=nm: e.dma_start(out=o_ffn[nm], in_=g.upst[:]),
             reads=[rn(g, "ups", l, c) for l in range(DEPTH) for c in range(NCH)], sem="misc", inc=16)
    S.wait_all("sp", ["st0", "st1", "st2", "st3", "misc", "vst"])

    from contextlib import ExitStack
    with ExitStack() as es:
        sems = {n: es.enter_context(nc.semaphore(n)) for n in sem_names}
        block = es.enter_context(nc.Block())

        def replay(engobj, prog):
            for it in prog:
                if it[0] == "wait":
                    engobj.wait_ge(sems[it[1]], it[2])
                else:
                    ins = it[1](engobj)
                    ins.then_inc(sems[it[2]], it[3])

        @block.sync
        def _(e):
            replay(e, S.progs["sp"])

        @block.gpsimd
        def _(e):
            replay(e, S.progs["pool"])

        @block.tensor
        def _(e):
            replay(e, S.progs["pe"])

        @block.scalar
        def _(e):
            replay(e, S.progs["act"])

        @block.vector
        def _(e):
            replay(e, S.progs["dve"])
    return nc


def _fm(a):
    a = np.asarray(a, np.float32)
    n = a.shape[-1] // 128
    a = a.reshape(a.shape[:-1] + (n, 128))
    return np.ascontiguousarray(np.moveaxis(a, -1, 0))


def prep_shared(inp):
    f = lambda k: np.asarray(inp[k], np.float32)
    w_in, w_out, w_up, w_down = f("w_in"), f("w_out"), f("w_up"), f("w_down")
    sh = {}
    cols = [0, 1, 2, 3, 8, 9, 10, 11, 12, 13, 14, 15]
    wi = w_in.reshape(DEPTH, KC, 128, 16, 128)
    sh["w_in_r"] = np.ascontiguousarray(wi[:, :, :, cols, :].transpose(0, 3, 2, 1, 4)).reshape(DEPTH, 12, 128, KC * 128)
    wv = w_in[:, :, 512:1024].reshape(DEPTH, KC, 128, 2, 256)
    sh["w_v_r"] = np.ascontiguousarray(wv.transpose(0, 3, 2, 1, 4)).reshape(DEPTH, 2, 128, KC * 256)
    wo = w_out.reshape(DEPTH, KC, 128, 8, 128)
    sh["w_out_r"] = np.ascontiguousarray(wo.transpose(0, 3, 2, 1, 4)).reshape(DEPTH, 8, 128, KC * 128)
    wu = w_up.reshape(DEPTH, KC, 128, 2, NJ, 128)
    sh["w_up_r"] = np.ascontiguousarray(wu.transpose(0, 4, 2, 1, 3, 5)).reshape(DEPTH, NJ, 128, KC * 256)
    wd = w_down.reshape(DEPTH, NJ, 128, 8, 128)
    sh["w_dn_r"] = np.ascontiguousarray(wd.transpose(0, 3, 2, 1, 4)).reshape(DEPTH, 8, 128, NJ * 128)
    sh["wsT"] = np.ascontiguousarray(f("w_s").transpose(3, 0, 1, 2)).reshape(128, DEPTH * 4, 128)
    k = np.arange(128)
    sh["mask"] = (k[:, None] <= k[None, :]).astype(np.float32)
    bs = f("b_s").reshape(1, DEPTH * 4 * 128)
    sh["bs2"] = np.ascontiguousarray(np.concatenate([bs, bs], 0))
    wp = f("w_pool")
    wpb = np.zeros((128, DEPTH, 2, 128), np.float32)
    for j in range(2):
        for half in range(2):
            wpb[half * 64:(half + 1) * 64, :, j, half * 64:(half + 1) * 64] = wp[:, 2 * j + half].transpose(1, 0, 2)
    sh["wpool"] = wpb.reshape(128, DEPTH * 2, 128)
    sh["ident"] = np.eye(128, dtype=np.float32)
    return sh


def prep_cst(inp, seq_start):
    f = lambda k: np.asarray(inp[k], np.float32)
    c = np.zeros((128, NCST), np.float32)
    c[:, O_G1:O_G1 + DEPTH * KC] = _fm(f("norm1_g")).reshape(128, -1)
    c[:, O_G2:O_G2 + DEPTH * KC] = _fm(f("norm2_g")).reshape(128, -1)
    c[:, O_FG:O_FG + KC] = _fm(f("final_g")).reshape(128, -1)
    c[:, O_PSC:O_PSC + DEPTH * 2] = _fm(f("pool_scale")).reshape(128, -1)
    c[:, O_WCV:O_WCV + DEPTH * 6] = _fm(f("w_conv")).transpose(0, 1, 3, 2).reshape(128, -1)
    c[:, O_WF:O_WF + DEPTH * NCH * 3] = _fm(f("w_fconv")).transpose(0, 1, 3, 2).reshape(128, -1)
    c[:, O_BF:O_BF + DEPTH * NCH] = _fm(f("b_fconv")).reshape(128, -1)
    win = np.array([[2, 4], [8, 16]], np.float32)
    t = np.arange(16, dtype=np.float32)
    for j in range(2):
        for half in range(2):
            sl = slice(half * 64, half * 64 + 64)
            c[sl, O_INVW + j] = 1.0 / win[j, half]
            cnt = np.minimum(t + 1, win[j, half]) if seq_start else np.full(16, win[j, half], np.float32)
            c[sl, O_INVC + 16 * j:O_INVC + 16 * j + 16] = 1.0 / cnt
    c[:, O_FLAG] = 0.0 if seq_start else 1.0
    c[:, O_EPS] = EPS
    c[0, O_M01] = 1.0
    c[1, O_M01 + 1] = 1.0
    return c


_CACHE = {}


def run(inp, main_tok, tiles, nseg):
    key = (main_tok, tuple(tiles))
    if key not in _CACHE:
        _CACHE[key] = build(main_tok, tiles)
    nc = _CACHE[key]
    sh = prep_shared(inp)
    xp = np.asarray(inp["x_prompt"], np.float32)
    xs = np.asarray(inp["x_sample"], np.float32)
    sp = np.asarray(inp["state_pool"], np.float32)
    sc = np.asarray(inp["state_conv"], np.float32)
    sf = np.asarray(inp["state_ffn_conv"], np.float32)
    in_maps = []
    for c in range(8):
        b, i = c // nseg, c % nseg
        t0 = i * main_tok
        xt = np.zeros((HALO + main_tok, D), np.float32)
        if i == 0:
            xt[HALO:] = xp[b, 0:main_tok]
        else:
            xt[:] = xp[b, t0 - HALO:t0 + main_tok]
        m = dict(sh)
        m["xT"] = np.ascontiguousarray(xt.reshape(-1, KC, 128).transpose(2, 1, 0))
        m["xsT"] = np.ascontiguousarray(xs[c].reshape(SAMP, KC, 128).transpose(2, 1, 0))
        m["cst"] = prep_cst(inp, i == 0)
        spool = np.zeros((128, DEPTH, 2, 16), np.float32)
        spool[:, :, :, 1:] = sp[:, c].reshape(DEPTH, 15, 2, 128).transpose(3, 0, 2, 1)
        m["spool"] = spool
        m["sconv"] = np.ascontiguousarray(sc[:, c].reshape(DEPTH, 2, 2, 128).transpose(3, 0, 2, 1))
        m["sffn"] = np.ascontiguousarray(sf[:, c].reshape(DEPTH, 2, NCH, 128).transpose(3, 0, 2, 1))
        in_maps.append(m)
    res = run_bass_kernel_spmd(nc, in_maps, core_ids=list(range(8)))
    R = res.results
    nb = 8 // nseg
    y_prompt = np.empty((nb, nseg * main_tok, D), np.float32)
    for c in range(8):
        b, i = c // nseg, c % nseg
        y_prompt[b, i * main_tok:(i + 1) * main_tok] = R[c]["yT"].transpose(2, 1, 0).reshape(main_tok, D)
    y_sample = np.stack([R[c]["ysT"].transpose(2, 1, 0).reshape(SAMP, D) for c in range(8)])

    def pool_out(a):
        return a[:, :, :, 1:].transpose(1, 3, 2, 0).reshape(DEPTH, 15, 256)

    def conv_out(a):
        return a.transpose(1, 3, 2, 0).reshape(DEPTH, 2, 256)

    def ffn_out(a):
        return a.transpose(1, 3, 2, 0).reshape(DEPTH, 2, 2 * DFF)

    lastc = [b * nseg + nseg - 1 for b in range(nb)]
    new_pool_p = np.stack([pool_out(R[c]["pool_p"]) for c in lastc], 1)
    new_conv_p = np.stack([conv_out(R[c]["conv_p"]) for c in lastc], 1)
    new_ffn_p = np.stack([ffn_out(R[c]["ffn_p"]) for c in lastc], 1)
    new_pool_s = np.stack([pool_out(R[c]["pool_s"]) for c in range(8)], 1)
    new_conv_s = np.stack([conv_out(R[c]["conv_s"]) for c in range(8)], 1)
    new_ffn_s = np.stack([ffn_out(R[c]["ffn_s"]) for c in range(8)], 1)
    new_v_s = np.stack([R[c]["v_s"].transpose(1, 0, 2) for c in range(8)], 1)
    outs = (y_prompt, y_sample, new_pool_p, new_conv_p, new_ffn_p, new_pool_s, new_conv_s, new_ffn_s, new_v_s)
    return tuple(np.ascontiguousarray(o, dtype=np.float32) for o in outs)


def kernel(**inputs):
    return run(inputs, 4096, [896] * 5, 4)
```
